# Optimizing a Trainium2 kernel written in Bass

```python
import jax, jax.numpy as jnp
from jax import lax
import numpy as np

D_MODEL = 1024
BATCH = 8
SEQ = 8192
DEPTH = 1

N_META = 16
GRID_W = 64
DN_HEADS = 4
DN_HEAD_DIM = 128
DN_WIDTH = DN_HEADS * DN_HEAD_DIM
DN_CHUNK = 64
CONV_W = 5
AT_Q_HEADS = 8
AT_KV_HEADS = 2
AT_HEAD_DIM = 64
AT_WIDTH = AT_Q_HEADS * AT_HEAD_DIM
AT_KV_WIDTH = AT_KV_HEADS * AT_HEAD_DIM
Q_BLOCK = 128
ROPE_THETA = 10000.0
ROPE_AXIS_DIM = AT_HEAD_DIM // 2
D_FF = 4 * D_MODEL
EPS = 1e-6
MIX_WIDTH = DN_WIDTH + AT_WIDTH
IN_SIZES = (DN_WIDTH, DN_WIDTH, DN_WIDTH, DN_WIDTH, 2 * DN_HEADS, 2 * DN_HEADS, AT_WIDTH, AT_KV_WIDTH, AT_KV_WIDTH)
IN_COLS = 4 * DN_WIDTH + 4 * DN_HEADS + AT_WIDTH + 2 * AT_KV_WIDTH

kernel_name = "hymba_deltanet_axial_gqa_encoder_block"


def rms_norm(x, w):
    xf = x.astype(jnp.float32)
    y = xf * lax.rsqrt(jnp.mean(xf * xf, axis=-1, keepdims=True) + EPS)
    return (y * w.astype(jnp.float32)).astype(x.dtype)


def l2_norm(x):
    xf = x.astype(jnp.float32)
    return xf * lax.rsqrt(jnp.sum(xf * xf, axis=-1, keepdims=True) + EPS)


def short_conv_silu(x, w):
    C = x.shape[-1]
    y = lax.conv_general_dilated(
        x, w[:, None, :].astype(x.dtype), window_strides=(1,),
        padding=[(CONV_W // 2, CONV_W // 2)],
        dimension_numbers=('NWC', 'WIO', 'NWC'), feature_group_count=C)
    return jax.nn.silu(y)


def to_scan_order(a_fwd, a_bwd):
    pad = jnp.zeros((a_fwd.shape[0], DN_CHUNK - N_META) + a_fwd.shape[2:], a_fwd.dtype)
    fwd = jnp.concatenate([pad, a_fwd], axis=1)
    bwd = jnp.concatenate([pad, a_bwd[:, :N_META], jnp.flip(a_bwd[:, N_META:], axis=1)], axis=1)
    return jnp.concatenate([fwd, bwd], axis=0)


def chunk_gated_delta_rule(q, k, v, g, beta):
    N, T, H, K = q.shape
    V = v.shape[-1]
    C = DN_CHUNK
    n = T // C

    def chunks(a):
        return jnp.moveaxis(a.reshape((N, n, C, H) + a.shape[3:]), 3, 2)

    q, k, v, g, beta = chunks(q), chunks(k), chunks(v), chunks(g), chunks(beta)
    gc = jnp.cumsum(g, axis=-1)
    incl = jnp.tril(jnp.ones((C, C), dtype=bool))
    strict = jnp.tril(jnp.ones((C, C), dtype=bool), -1)
    decay = jnp.exp(jnp.where(incl, gc[..., :, None] - gc[..., None, :], -jnp.inf))
    kk = jnp.einsum('bnhik,bnhjk->bnhij', k, k)
    a_low = jnp.where(strict, kk * decay * beta[..., :, None], 0.0)
    rhs = jnp.concatenate([v * beta[..., None], k * (beta * jnp.exp(gc))[..., None]], axis=-1)
    uw = lax.linalg.triangular_solve(a_low + jnp.eye(C, dtype=a_low.dtype), rhs,
                                     left_side=True, lower=True, unit_diagonal=True)
    u, w = uw[..., :V], uw[..., V:]
    qk = jnp.einsum('bnhik,bnhjk->bnhij', q, k) * decay
    q_dec = q * jnp.exp(gc)[..., None]
    k_dec = k * jnp.exp(gc[..., -1:] - gc)[..., None]
    g_last = jnp.exp(gc[..., -1])

    def step(S, xs):
        q_i, k_i, u_i, w_i, qk_i, gl_i = xs
        v_new = u_i - jnp.einsum('bhck,bhkv->bhcv', w_i, S)
        o_i = jnp.einsum('bhck,bhkv->bhcv', q_i, S) + jnp.einsum('bhij,bhjv->bhiv', qk_i, v_new)
        S = S * gl_i[..., None, None] + jnp.einsum('bhck,bhcv->bhkv', k_i, v_new)
        return S, o_i

    xs = tuple(jnp.moveaxis(a, 1, 0) for a in (q_dec, k_dec, u, w, qk, g_last))
    S0 = jnp.zeros((N, H, K, V), jnp.float32)
    _, o = lax.scan(step, S0, xs)
    o = jnp.moveaxis(o, 0, 1)
    return jnp.moveaxis(o, 3, 2).reshape(N, T, H, V)


def gated_deltanet_group(q, k, v, z, b, a, conv_w, a_log, dt_bias, o_norm_w):
    B, L, _ = q.shape
    out_dtype = q.dtype
    qkv = short_conv_silu(jnp.concatenate([q, k, v], axis=-1), conv_w)
    q, k, v = jnp.split(qkv, 3, axis=-1)
    heads = lambda t: t.reshape(B, L, DN_HEADS, DN_HEAD_DIM)
    q = l2_norm(heads(q)) * (DN_HEAD_DIM ** -0.5)
    k = l2_norm(heads(k))
    v = heads(v).astype(jnp.float32)
    beta = jax.nn.sigmoid(b.astype(jnp.float32)).reshape(B, L, 2, DN_HEADS)
    g = -jnp.exp(a_log.astype(jnp.float32)) * jax.nn.softplus(
        a.astype(jnp.float32).reshape(B, L, 2, DN_HEADS) + dt_bias.astype(jnp.float32))
    o = chunk_gated_delta_rule(
        to_scan_order(q, q), to_scan_order(k, k), to_scan_order(v, v),
        to_scan_order(g[:, :, 0], g[:, :, 1]), to_scan_order(beta[:, :, 0], beta[:, :, 1]))
    pad = DN_CHUNK - N_META
    o_f = o[:B, pad:]
    o_b = o[B:, pad:]
    o_b = jnp.concatenate([o_b[:, :N_META], jnp.flip(o_b[:, N_META:], axis=1)], axis=1)
    o = o_f + o_b
    o = rms_norm(o, o_norm_w) * jax.nn.silu(heads(z).astype(jnp.float32))
    return o.reshape(B, L, DN_WIDTH).astype(out_dtype)


def axial_rope_angles(n_real):
    rows = n_real // GRID_W
    r, c = jnp.meshgrid(jnp.arange(rows), jnp.arange(GRID_W), indexing='ij')
    r = r.reshape(-1).astype(jnp.float32)
    c = c.reshape(-1).astype(jnp.float32)
    F = ROPE_AXIS_DIM // 2
    freqs = ROPE_THETA ** (-jnp.arange(F, dtype=jnp.float32) / F)
    ang = jnp.concatenate([r[:, None] * freqs, c[:, None] * freqs], axis=-1)
    return jnp.concatenate([jnp.zeros((N_META, 2 * F), jnp.float32), ang], axis=0)


def apply_axial_rope(x, ang):
    B, L, H, D = x.shape
    F = ROPE_AXIS_DIM // 2
    xr = x.astype(jnp.float32).reshape(B, L, H, 2, 2, F)
    x1, x2 = xr[..., 0, :], xr[..., 1, :]
    an = ang.reshape(L, 1, 2, F)
    cos, sin = jnp.cos(an), jnp.sin(an)
    out = jnp.stack([x1 * cos - x2 * sin, x2 * cos + x1 * sin], axis=-2)
    return out.reshape(B, L, H, D).astype(x.dtype)


def axial_gqa_group(q, k, v, q_norm_w, k_norm_w):
    B, L, _ = q.shape
    G = AT_Q_HEADS // AT_KV_HEADS
    n_real = L - N_META
    q = rms_norm(q.reshape(B, L, AT_Q_HEADS, AT_HEAD_DIM), q_norm_w)
    k = rms_norm(k.reshape(B, L, AT_KV_HEADS, AT_HEAD_DIM), k_norm_w)
    v = v.reshape(B, L, AT_KV_HEADS, AT_HEAD_DIM)
    ang = axial_rope_angles(n_real)
    q = apply_axial_rope(q, ang) * (AT_HEAD_DIM ** -0.5)
    k = apply_axial_rope(k, ang)
    q = q.reshape(B, L, AT_KV_HEADS, G, AT_HEAD_DIM)

    def attend(qb):
        s = jnp.einsum('bqhgd,bkhd->bhgqk', qb, k, preferred_element_type=jnp.float32)
        p = jax.nn.softmax(s, axis=-1).astype(v.dtype)
        return jnp.einsum('bhgqk,bkhd->bqhgd', p, v)

    o_meta = attend(q[:, :N_META]).reshape(B, N_META, AT_WIDTH)
    nb = n_real // Q_BLOCK
    qr = q[:, N_META:].reshape(B, nb, Q_BLOCK, AT_KV_HEADS, G, AT_HEAD_DIM)
    o_real = lax.map(attend, jnp.moveaxis(qr, 1, 0))
    o_real = jnp.moveaxis(o_real, 0, 1).reshape(B, n_real, AT_WIDTH)
    return jnp.concatenate([o_meta, o_real], axis=1)


def setup_inputs(seed: int = 0) -> dict:
    key = jax.random.key(seed)
    ks = jax.random.split(key, 20)
    f32 = jnp.float32
    nrm = lambda k_, shape, scale: jax.random.normal(k_, shape, f32) * scale
    gain = lambda k_, shape: 1.0 + 0.02 * jax.random.normal(k_, shape, f32)
    x = jax.random.normal(ks[0], (BATCH, SEQ, D_MODEL), f32)
    meta_tokens = nrm(ks[1], (N_META, D_MODEL), 1.0)
    w_in = nrm(ks[2], (DEPTH, D_MODEL, IN_COLS), D_MODEL ** -0.5)
    conv_w = nrm(ks[3], (DEPTH, CONV_W, 3 * DN_WIDTH), CONV_W ** -0.5)
    a_log = jnp.log(jax.random.uniform(ks[4], (DEPTH, 2, DN_HEADS), f32, 1.0, 16.0))
    dt = jnp.exp(jax.random.uniform(ks[5], (DEPTH, 2, DN_HEADS), f32, np.log(1e-3), np.log(1e-1)))
    dt_bias = dt + jnp.log(-jnp.expm1(-dt))
    dn_out_norm = gain(ks[6], (DEPTH, DN_HEAD_DIM))
    q_norm = gain(ks[7], (DEPTH, AT_HEAD_DIM))
    k_norm = gain(ks[8], (DEPTH, AT_HEAD_DIM))
    w_out = nrm(ks[9], (DEPTH, MIX_WIDTH, D_MODEL), MIX_WIDTH ** -0.5)
    norm_mix_pre = gain(ks[10], (DEPTH, D_MODEL))
    norm_mix_post = gain(ks[11], (DEPTH, D_MODEL))
    w_up = nrm(ks[12], (DEPTH, D_MODEL, D_FF), D_MODEL ** -0.5)
    w_down = nrm(ks[13], (DEPTH, D_FF, D_MODEL), D_FF ** -0.5)
    norm_mlp_pre = gain(ks[14], (DEPTH, D_MODEL))
    norm_mlp_post = gain(ks[15], (DEPTH, D_MODEL))
    return {"x": x, "meta_tokens": meta_tokens, "w_in": w_in, "conv_w": conv_w,
            "a_log": a_log, "dt_bias": dt_bias, "dn_out_norm": dn_out_norm,
            "q_norm": q_norm, "k_norm": k_norm, "w_out": w_out,
            "norm_mix_pre": norm_mix_pre, "norm_mix_post": norm_mix_post,
            "w_up": w_up, "w_down": w_down,
            "norm_mlp_pre": norm_mlp_pre, "norm_mlp_post": norm_mlp_post}


def reference(x, meta_tokens, w_in, conv_w, a_log, dt_bias, dn_out_norm, q_norm, k_norm, w_out,
              norm_mix_pre, norm_mix_post, w_up, w_down, norm_mlp_pre, norm_mlp_post):
    B = x.shape[0]
    split_points = [int(s) for s in np.cumsum(IN_SIZES)[:-1]]
    meta = jnp.broadcast_to(meta_tokens[None].astype(x.dtype), (B, N_META, D_MODEL))
    h = jnp.concatenate([meta, x], axis=1)
    for l in range(DEPTH):
        u = rms_norm(h, norm_mix_pre[l])
        proj = u @ w_in[l]
        dq, dk, dv, dz, db, da, aq, ak, av = jnp.split(proj, split_points, axis=-1)
        o_dn = gated_deltanet_group(dq, dk, dv, dz, db, da, conv_w[l], a_log[l], dt_bias[l], dn_out_norm[l])
        o_at = axial_gqa_group(aq, ak, av, q_norm[l], k_norm[l])
        mix = jnp.concatenate([o_dn, o_at.astype(o_dn.dtype)], axis=-1) @ w_out[l]
        h = h + rms_norm(mix, norm_mix_post[l])
        u = rms_norm(h, norm_mlp_pre[l])
        f = jnp.square(jax.nn.relu(u @ w_up[l])) @ w_down[l]
        h = h + rms_norm(f, norm_mlp_post[l])
    return h[:, N_META:]
```

```python
import os
import numpy as np
from contextlib import ExitStack
import concourse.bass as bass
import concourse.mybir as mybir
from concourse.bass_utils import run_bass_kernel_spmd

F32 = mybir.dt.float32
BF16 = mybir.dt.bfloat16
AF = mybir.ActivationFunctionType
ALU = mybir.AluOpType
AX = mybir.AxisListType

D_MODEL = 1024
N_META = 16
DN_HEADS = 4
DH = 128
DN_W = 512
AT_W = 512
IN_COLS = 2832
D_FF = 4096
EPS = 1e-6
CH = 64


class _Op:
    __slots__ = ("idx", "eng", "fn", "deps", "is_dma", "dkey", "sig", "needed")

    def __init__(self, idx, eng, fn, is_dma, dkey):
        self.idx = idx
        self.eng = eng
        self.fn = fn
        self.deps = set()
        self.is_dma = is_dma
        self.dkey = dkey
        self.sig = None
        self.needed = False


class Sched:
    SEM_ROT = 30000

    def __init__(self, nc, es):
        self.nc = nc
        self.es = es
        self.ops = []
        self.res_w = {}
        self.res_r = {}
        self.engs = {"pe": nc.tensor, "act": nc.scalar, "dve": nc.vector,
                     "pool": nc.gpsimd, "sp": nc.sync}

    def add(self, eng, fn, reads=(), writes=(), dma=False, dkey=None):
        idx = len(self.ops)
        op = _Op(idx, eng, fn, dma, dkey)
        for r in reads:
            w = self.res_w.get(r)
            if w is not None:
                op.deps.add(w)
        for wr in writes:
            w = self.res_w.get(wr)
            if w is not None:
                op.deps.add(w)
            for rd in self.res_r.get(wr, ()):
                op.deps.add(rd)
        op.deps.discard(idx)
        for r in reads:
            self.res_r.setdefault(r, []).append(idx)
        for wr in writes:
            self.res_w[wr] = idx
            self.res_r[wr] = []
        self.ops.append(op)
        return idx

    def dma(self, q, out, in_, reads, writes, dkey):
        return self.add(q, lambda e: e.dma_start(out=out, in_=in_), reads, writes,
                        dma=True, dkey=dkey)

    def phase_end(self, final=False):
        ops = self.ops
        if not hasattr(self, "eng_cnt"):
            self.eng_cnt = {}
            self.dma_cnt = {}
            self.sems = {}
            self.waited = {}
            self.barrier = []
        for op in ops:
            keep = {}
            for d in op.deps:
                dop = ops[d]
                if dop.is_dma:
                    k = ("dma", dop.dkey)
                else:
                    if dop.eng == op.eng and not op.is_dma and op.eng == "pe":
                        continue
                    k = ("eng", dop.eng)
                if k not in keep or keep[k] < d:
                    keep[k] = d
            op.deps = set(keep.values())
            for d in op.deps:
                ops[d].needed = True
        last = {}
        for op in ops:
            if op.fn is None:
                continue
            k = ("dma", op.dkey) if op.is_dma else ("eng", op.eng)
            last[k] = op
        for op in last.values():
            op.needed = True

        def get_sem(key):
            if key not in self.sems:
                self.sems[key] = self.es.enter_context(
                    self.nc.semaphore("s%d" % len(self.sems)))
            return self.sems[key]

        for op in ops:
            if not op.needed:
                continue
            if op.is_dma:
                c = self.dma_cnt.get(op.dkey, 0) + 1
                self.dma_cnt[op.dkey] = c
                rot, val = divmod(c - 1, self.SEM_ROT // 16)
                op.sig = (get_sem(("dma", op.dkey, rot)), (val + 1) * 16, 16)
            else:
                c = self.eng_cnt.get(op.eng, 0) + 1
                self.eng_cnt[op.eng] = c
                rot, val = divmod(c - 1, self.SEM_ROT)
                op.sig = (get_sem(("eng", op.eng, rot)), val + 1, 1)
        waited = self.waited

        def wait(eng, sem, val):
            k = (eng, id(sem))
            if waited.get(k, 0) < val:
                self.engs[eng].wait_ge(sem, val)
                waited[k] = val

        started = set()
        for op in ops:
            e = self.engs[op.eng]
            if op.eng not in started:
                started.add(op.eng)
                for sem, val in self.barrier:
                    wait(op.eng, sem, val)
            for d in sorted(op.deps):
                sem, val, _ = ops[d].sig
                wait(op.eng, sem, val)
            if op.fn is None:
                continue
            inst = op.fn(e)
            if op.sig is not None:
                inst.then_inc(op.sig[0], op.sig[2])
        self.barrier = self.barrier + [(op.sig[0], op.sig[1]) for op in last.values()]
        mx = {}
        for sem, val in self.barrier:
            if id(sem) not in mx or mx[id(sem)][1] < val:
                mx[id(sem)] = (sem, val)
        self.barrier = list(mx.values())
        if final:
            for sem, val in self.barrier:
                wait("sp", sem, val)
        self.n_ops = getattr(self, "n_ops", 0) + len(ops)
        self.ops = []
        self.res_w = {}
        self.res_r = {}


class Ctx:
    pass


def _alloc(C, es, kind, name, shape, dt):
    if kind == "sb":
        return es.enter_context(C.nc.sbuf_tensor(name, list(shape), dt))
    return es.enter_context(C.nc.psum_tensor(name, list(shape), dt))


def build(SEQ, debug=False, phases="ABGCDE"):
    nc = bass.Bass("TRN2", target_bir_lowering=False)
    C = Ctx()
    C.nc = nc
    C.SEQ = SEQ
    C.NE2 = 64 + SEQ
    C.NB = C.NE2 // CH
    C.LK = N_META + SEQ
    C.NQB = SEQ // 128
    C.NKT = 1 + SEQ // 128
    C.debug = debug

    def din(name, shape, dt=F32):
        return nc.dram_tensor(name, list(shape), dt, kind="ExternalInput").ap()

    def dscr(name, shape, dt=F32):
        kind = "ExternalOutput" if debug else "Internal"
        return nc.dram_tensor(name, list(shape), dt, kind=kind).ap()

    D = Ctx()
    C.D = D
    D.x = din("x", [SEQ, D_MODEL])
    D.meta = din("meta", [N_META, D_MODEL])
    D.w_in = din("w_in", [128, 8, IN_COLS])
    D.nmix_pre = din("nmix_pre", [128, 8])
    D.conv_w = din("conv_w", [128, 12, 5])
    D.a_log = din("a_log", [128, 8])
    D.dt_bias = din("dt_bias", [128, 8])
    D.dn_norm = din("dn_norm", [128, 512])
    D.q_norm = din("q_norm", [128, 512])
    D.k_norm = din("k_norm", [128, 128])
    D.w_out_dn = din("w_out_dn", [128, 4, D_MODEL])
    D.w_out_at = din("w_out_at", [64, 8, D_MODEL])
    D.nmix_post = din("nmix_post", [128, D_MODEL])
    D.nmlp_pre = din("nmlp_pre", [128, 8])
    D.nmlp_post = din("nmlp_post", [128, D_MODEL])
    D.w_up = din("w_up", [128, 8, D_FF])
    D.w_down = din("w_down", [128, 32, D_MODEL])
    D.ident = din("ident", [128, 128])
    D.rope_cos = din("rope_cos", [SEQ, 256])
    D.rope_sin = din("rope_sin", [SEQ, 256])
    D.masks = din("masks", [64, 12, 64])
    D.out = nc.dram_tensor("out", [SEQ, D_MODEL], F32, kind="ExternalOutput").ap()
    D.PQKV = dscr("PQKV", [1536, C.NE2 + 4], BF16)
    D.ZS = dscr("ZS", [SEQ, 512], F32)
    D.BG = dscr("BG", [C.NE2, 16], F32)
    D.QT = dscr("QT", [2, 64, C.NQB, 4, 128], BF16)
    D.QKT = dscr("QKT", [2, 4, 128, C.NE2], BF16)
    D.KVT = dscr("KVT", [C.NE2, 2, 4, 128], BF16)
    D.OF = dscr("OF", [2, C.NE2, 512], F32)
    D.OAT = dscr("OAT", [2, 64, C.NQB, 4, 128], BF16)
    D.H1 = dscr("H1", [SEQ, D_MODEL], F32)
    if debug:
        D.KTd = dscr("KTd", [64, 2, C.LK], BF16)
        D.VAd = dscr("VAd", [128, C.NKT, 2, 128], BF16)

    with ExitStack() as es:
        S = Sched(nc, es)
        C.S = S
        C.ident_f = _alloc(C, es, "sb", "ident_f", [128, 128], F32)
        C.ident_b = _alloc(C, es, "sb", "ident_b", [128, 128], BF16)
        C.eps_t = _alloc(C, es, "sb", "eps_t", [128, 1], F32)
        C.one_t = _alloc(C, es, "sb", "one_t", [128, 1], F32)
        S.dma("sp", C.ident_f[:], D.ident[:, :], [], ["ident_f"], "ident_f")
        S.add("dve", lambda e: e.tensor_copy(C.ident_b[:], C.ident_f[:]), ["ident_f"], ["ident_b"])
        S.add("pool", lambda e: e.memset(C.eps_t[:], EPS), [], ["eps_t"])
        S.add("pool", lambda e: e.memset(C.one_t[:], 1.0), [], ["one_t"])
        with ExitStack() as kes:
            C.KT = _alloc(C, kes, "sb", "KT", [64, 2, C.LK], BF16)
            C.VA = _alloc(C, kes, "sb", "VA", [128, C.NKT, 2, 128], BF16)
            S.add("pool", lambda e: e.memset(C.VA[:], 1.0), [], ["VA"])
            S.phase_end()
            for ph, fn in (("A", phase_A), ("B", phase_B), ("G", phase_G), ("C", phase_C)):
                if ph in phases:
                    with ExitStack() as pes:
                        fn(C, pes)
                        S.phase_end()
            if debug:
                S.dma("pool", D.KTd[:, :, :], C.KT[:], ["KT"], ["KTd"], "KTd")
                S.dma("pool", D.VAd[:, :, :, :], C.VA[:], ["VA"], ["VAd"], "VAd")
                S.phase_end()
        for ph, fn in (("D", phase_D1), ("E", phase_D2)):
            if ph in phases:
                with ExitStack() as pes:
                    fn(C, pes)
                    S.phase_end()
        S.phase_end(final=True)
    return nc


def phase_A(C, es):
    nc, S, D = C.nc, C.S, C.D
    SEQ = C.SEQ
    sb = lambda n, s, d=F32: _alloc(C, es, "sb", n, s, d)
    ps = lambda n, s, d=F32: _alloc(C, es, "ps", n, s, d)
    Wb = sb("A_Wb", [128, 8, IN_COLS], BF16)
    wst = sb("A_wst", [128, IN_COLS], F32)
    nw = sb("A_nw", [128, 8], F32)
    dtb = sb("A_dtb", [128, 8], F32)
    negA = sb("A_negA", [128, 8], F32)
    qnw = sb("A_qnw", [128, 512], F32)
    knw = sb("A_knw", [128, 128], F32)
    xt = [sb("A_xt%d" % i, [128, 1024], F32) for i in range(2)]
    junk = sb("A_junk", [128, 1024], F32)
    ss = sb("A_ss", [128, 1], F32)
    rstd = sb("A_rstd", [128, 1], F32)
    xn = sb("A_xn", [128, 1024], BF16)
    xnT = sb("A_xnT", [128, 8, 512], BF16)
    qkv = sb("A_qkv", [128, 12, 512], BF16)
    zs = sb("A_zs", [128, 512], F32)
    bg = sb("A_bg", [128, 16], F32)
    t8 = sb("A_t8", [128, 8], F32)
    cos = sb("A_cos", [128, 256], F32)
    sin = sb("A_sin", [128, 256], F32)
    sq = sb("A_sq", [128, 512], F32)
    ssq = sb("A_ssq", [128, 16], F32)
    qn = sb("A_qn", [128, 512], F32)
    ra = sb("A_ra", [128, 256], F32)
    rb = sb("A_rb", [128, 256], F32)
    qr = sb("A_qr", [128, 512], BF16)
    kn = sb("A_kn", [128, 128], F32)
    kr = sb("A_kr", [128, 128], BF16)
    qt = sb("A_qt", [64, 8, 128], BF16)
    zero = sb("A_zero", [128, 12, 52], BF16)
    zero32 = sb("A_zero32", [64, 16], F32)
    pT = ps("A_pT", [128, 8, 128], BF16)
    psF = [ps("A_psF%d" % i, [128, 512], F32) for i in range(2)]
    psZ = ps("A_psZ", [128, 512], F32)
    psQ = ps("A_psQ", [128, 512], F32)
    psK = ps("A_psK", [128, 512], F32)
    psQT = ps("A_psQT", [64, 8, 128], BF16)
    psKT = ps("A_psKT", [64, 2, 128], BF16)

    S.dma("sp", nw[:], D.nmix_pre[:, :], [], ["nw"], "nw")
    S.dma("sp", dtb[:], D.dt_bias[:, :], [], ["dtb"], "dtb")
    S.dma("sp", negA[:], D.a_log[:, :], [], ["negA"], "negA")
    S.dma("sp", qnw[:], D.q_norm[:, :], [], ["qnw"], "qnw")
    S.dma("sp", knw[:], D.k_norm[:, :], [], ["knw"], "knw")
    S.add("act", lambda e: e.activation(out=negA[:], in_=negA[:], func=AF.Exp), ["negA"], ["negA"])
    S.add("dve", lambda e: e.tensor_scalar(out=negA[:], in0=negA[:], scalar1=-1.0, scalar2=None,
                                           op0=ALU.mult), ["negA"], ["negA"])
    for kc in range(8):
        S.dma("sp", wst[:], D.w_in[:, kc, :], [], ["wst"], "wst")
        S.add("dve", lambda e, kc=kc: e.tensor_scalar(out=Wb[:, kc, :], in0=wst[:], scalar1=nw[:, kc:kc + 1],
                                                      scalar2=None, op0=ALU.mult),
              ["wst", "nw"], ["Wb"])
    S.add("pool", lambda e: e.memset(zero[:], 0.0), [], ["zero"])
    S.add("pool", lambda e: e.memset(zero32[:], 0.0), [], ["zero32"])
    pq = D.PQKV.rearrange("(c p) n -> p c n", p=128)
    S.dma("pool", pq[:, :, 0:50], zero[:, :, 0:50], ["zero"], ["PQKV"], "zero")
    S.dma("pool", pq[:, :, C.NE2 + 2:C.NE2 + 4], zero[:, :, 50:52], ["zero"], ["PQKV"], "zero")
    S.dma("pool", D.BG[0:48, :], zero32[0:48, :], ["zero32"], ["BG"], "zero32")

    def norm_head(psrc, nh, dst_n, w_bc, rkey):
        P = rkey
        W = nh * 64
        S.add("act", lambda e: e.activation(out=sq[0:P, 0:W], in_=psrc, func=AF.Square),
              ["psK" if nh == 2 else "psQ"], ["sq"])
        S.add("dve", lambda e: e.tensor_reduce(out=ssq[0:P, 0:nh],
                                               in_=sq[0:P, 0:W].rearrange("p (h d) -> p h d", d=64),
                                               op=ALU.add, axis=AX.X), ["sq"], ["ssq"])
        S.add("act", lambda e: e.activation(out=ssq[0:P, 0:nh], in_=ssq[0:P, 0:nh], func=AF.Sqrt,
                                            scale=1.0 / 64, bias=C.eps_t[0:P, :]), ["ssq"], ["ssq"])
        S.add("dve", lambda e: e.reciprocal(out=ssq[0:P, 0:nh], in_=ssq[0:P, 0:nh]), ["ssq"], ["ssq"])
        S.add("dve", lambda e: e.tensor_tensor(
            out=dst_n[0:P, 0:W].rearrange("p (h d) -> p h d", d=64),
            in0=psrc.rearrange("p (h d) -> p h d", d=64),
            in1=ssq[0:P, 0:nh].unsqueeze(2).broadcast_to([P, nh, 64]), op=ALU.mult),
            ["ssq", "psK" if nh == 2 else "psQ"], ["dstn%d" % nh])
        S.add("dve", lambda e: e.tensor_tensor(out=dst_n[0:P, 0:W], in0=dst_n[0:P, 0:W],
                                               in1=w_bc[0:P, 0:W], op=ALU.mult),
              ["dstn%d" % nh, "qnw", "knw"], ["dstn%d" % nh])

    def rope(src, dst, nh, P, do_rope):
        W = nh * 64
        key = "dstn%d" % nh
        okey = "rot%d" % nh
        if not do_rope:
            S.add("dve", lambda e: e.tensor_copy(dst[0:P, 0:W], src[0:P, 0:W]), [key], [okey])
            return
        H2 = nh * 2
        sv = src[0:P, 0:W].rearrange("p (a t f) -> p a t f", t=2, f=16)
        dv = dst[0:P, 0:W].rearrange("p (a t f) -> p a t f", t=2, f=16)
        x1, x2 = sv[:, :, 0, :], sv[:, :, 1, :]
        o1, o2 = dv[:, :, 0, :], dv[:, :, 1, :]
        cv = cos[0:P, 0:H2 * 16].rearrange("p (a f) -> p a f", f=16)
        sn = sin[0:P, 0:H2 * 16].rearrange("p (a f) -> p a f", f=16)
        av = ra[0:P, 0:H2 * 16].rearrange("p (a f) -> p a f", f=16)
        bv = rb[0:P, 0:H2 * 16].rearrange("p (a f) -> p a f", f=16)
        tt = lambda o, a, b, op: (lambda e: e.tensor_tensor(out=o, in0=a, in1=b, op=op))
        S.add("dve", tt(av, x1, cv, ALU.mult), [key, "cos"], ["ra"])
        S.add("dve", tt(bv, x2, sn, ALU.mult), [key, "sin"], ["rb"])
        S.add("dve", tt(o1, av, bv, ALU.subtract), ["ra", "rb"], [okey])
        S.add("dve", tt(av, x2, cv, ALU.mult), [key, "cos", okey], ["ra"])
        S.add("dve", tt(bv, x1, sn, ALU.mult), [key, "sin", okey], ["rb"])
        S.add("dve", tt(o2, av, bv, ALU.add), ["ra", "rb"], [okey])

    def do_tile(ti, xsrc, P, c0, r0, k0, kt):
        b = ti % 2
        X = xt[b]
        xk = "xt%d" % b
        S.dma("sp", X[0:P, :], xsrc, [], [xk], xk)
        S.add("act", lambda e: e.activation(out=junk[0:P, :], in_=X[0:P, :], func=AF.Square,
                                            accum_out=ss[0:P, :]), [xk], ["junk", "ss"])
        S.add("act", lambda e: e.activation(out=ss[0:P, :], in_=ss[0:P, :], func=AF.Sqrt,
                                            scale=1.0 / D_MODEL, bias=C.eps_t[0:P, :]), ["ss"], ["ss"])
        S.add("dve", lambda e: e.reciprocal(out=rstd[0:P, :], in_=ss[0:P, :]), ["ss"], ["rstd"])
        S.add("act", lambda e: e.activation(out=xn[0:P, :], in_=X[0:P, :], func=AF.Copy,
                                            scale=rstd[0:P, 0:1]), [xk, "rstd"], ["xn"])
        for kc in range(8):
            S.add("pe", lambda e, kc=kc: e.transpose(pT[:, kc, 0:P], xn[0:P, kc * 128:(kc + 1) * 128],
                                                     C.ident_b[0:P, 0:P]), ["xn", "ident_b"], ["pT"])
        S.add("dve", lambda e: e.tensor_copy(xnT[:, :, c0:c0 + P], pT[:, :, 0:P]), ["pT"], ["xnT"])
        def tm(dst, col0, ncol, key, first=True):
            for kc in range(8):
                S.add("pe", lambda e, kc=kc: e.matmul(dst, xnT[:, kc, c0:c0 + P], Wb[:, kc, col0:col0 + ncol],
                                                      start=(kc == 0), stop=(kc == 7)),
                      ["xnT", "Wb"], [key])
        if r0 is not None:
            tm(psZ[0:P, :], 1536, 512, "psZ")
            S.add("act", lambda e: e.activation(out=zs[0:P, :], in_=psZ[0:P, :], func=AF.Silu), ["psZ"], ["zs"])
            S.dma("pool", D.ZS[r0:r0 + P, :], zs[0:P, :], ["zs"], ["ZS"], "zs")
            tm(psQ[0:P, :], 2064, 512, "psQ")
        tm(psK[0:P, 0:256], 2576, 256, "psK")
        tm(psK[0:P, 256:272], 2048, 16, "psK")
        S.add("act", lambda e: e.activation(out=bg[0:P, 0:8], in_=psK[0:P, 256:264], func=AF.Sigmoid),
              ["psK"], ["bg"])
        S.add("dve", lambda e: e.tensor_tensor(out=t8[0:P, :], in0=psK[0:P, 264:272], in1=dtb[0:P, :],
                                               op=ALU.add), ["psK", "dtb"], ["t8"])
        S.add("act", lambda e: e.activation(out=t8[0:P, :], in_=t8[0:P, :], func=AF.Exp), ["t8"], ["t8"])
        S.add("act", lambda e: e.activation(out=t8[0:P, :], in_=t8[0:P, :], func=AF.Ln, bias=C.one_t[0:P, :]),
              ["t8"], ["t8"])
        S.add("dve", lambda e: e.tensor_tensor(out=bg[0:P, 8:16], in0=t8[0:P, :], in1=negA[0:P, :],
                                               op=ALU.mult), ["t8", "negA", "bg"], ["bg"])
        e2 = 48 + k0
        S.dma("pool", D.BG[e2:e2 + P, :], bg[0:P, :], ["bg"], ["BG"], "bg")
        if r0 is not None:
            S.dma("sp", cos[0:P, :], D.rope_cos[r0:r0 + P, :], [], ["cos"], "cos")
            S.dma("sp", sin[0:P, :], D.rope_sin[r0:r0 + P, :], [], ["sin"], "sin")
            norm_head(psQ[0:P, :], 8, qn, qnw, P)
            rope(qn, qr, 8, P, True)
            for h in range(8):
                S.add("pe", lambda e, h=h: e.transpose(psQT[:, h, 0:P], qr[0:P, h * 64:(h + 1) * 64],
                                                       C.ident_b[0:P, 0:P]), ["rot8", "ident_b"], ["psQT"])
            S.add("act", lambda e: e.activation(out=qt[:, :, 0:P], in_=psQT[:, :, 0:P], func=AF.Copy),
                  ["psQT"], ["qt"])
            qb = r0 // 128
            S.dma("pool", D.QT[:, :, qb, :, :].rearrange("k d g q -> d k g q"),
                  qt[:].rearrange("d (k g) q -> d k g q", k=2), ["qt"], ["QT"], "qt")
        norm_head(psK[0:P, 0:128], 2, kn, knw, P)
        rope(kn, kr, 2, P, r0 is not None)
        for h in range(2):
            S.add("pe", lambda e, h=h: e.transpose(psKT[:, h, 0:P], kr[0:P, h * 64:(h + 1) * 64],
                                                   C.ident_b[0:P, 0:P]), ["rot2", "ident_b"], ["psKT"])
        S.add("act", lambda e: e.activation(out=C.KT[:, :, k0:k0 + P], in_=psKT[:, :, 0:P], func=AF.Copy),
              ["psKT"], ["KT"])
        S.add("act", lambda e: e.activation(
            out=C.VA[0:P, kt, :, 0:64], in_=psK[0:P, 128:256].rearrange("p (h d) -> p h d", d=64),
            func=AF.Copy), ["psK"], ["VA"])

    def feat_major(T, e2_0):
        for fc in range(12):
            pf = psF[fc % 2]
            pk = "psF%d" % (fc % 2)
            for kc in range(8):
                S.add("pe", lambda e, kc=kc, fc=fc, pf=pf: e.matmul(
                    pf[:, 0:T], Wb[:, kc, fc * 128:(fc + 1) * 128], xnT[:, kc, 0:T],
                    start=(kc == 0), stop=(kc == 7)), ["xnT", "Wb"], [pk])
            if fc % 2 == 0:
                S.add("act", lambda e, fc=fc, pf=pf: e.activation(out=qkv[:, fc, 0:T], in_=pf[:, 0:T],
                                                                  func=AF.Copy), [pk], ["qkv"])
            else:
                S.add("dve", lambda e, fc=fc, pf=pf: e.tensor_copy(qkv[:, fc, 0:T], pf[:, 0:T]),
                      [pk], ["qkv"])
        S.dma("pool", pq[:, :, 2 + e2_0:2 + e2_0 + T], qkv[:, :, 0:T], ["qkv"], ["PQKV"], "qkv")

    do_tile(0, D.meta[:, :], N_META, 0, None, 0, 0)
    feat_major(N_META, 48)
    ti = 1
    for st in range(SEQ // 512):
        for i in range(4):
            r0 = st * 512 + i * 128
            do_tile(ti, D.x[r0:r0 + 128, :], 128, i * 128, r0, N_META + r0, 1 + r0 // 128)
            ti += 1
        feat_major(512, 64 + st * 512)


def phase_B(C, es):
    nc, S, D = C.nc, C.S, C.D
    NE2 = C.NE2
    sb = lambda n, s, d=F32: _alloc(C, es, "sb", n, s, d)
    ps = lambda n, s, d=F32: _alloc(C, es, "ps", n, s, d)
    cw = sb("B_cw", [128, 12, 5], F32)
    Dg = sb("B_Dg", [128, 12, 5, 128], BF16)
    ones = sb("B_ones", [128, 128], BF16)
    pin = [sb("B_pin%d" % i, [128, 12, 516], BF16) for i in range(2)]
    ysb = sb("B_ysb", [128, 8, 512], F32)
    ssb = sb("B_ssb", [128, 8, 512], F32)
    sqb = [sb("B_sqb%d" % i, [128, 512], BF16) for i in range(2)]
    qk_o = sb("B_qko", [128, 8, 512], BF16)
    kv_f = sb("B_kvf", [128, 8, 512], BF16)
    kv_t = [sb("B_kvt%d" % i, [128, 8, 128], BF16) for i in range(2)]
    psC = [ps("B_psC%d" % i, [128, 512], F32) for i in range(2)]
    psN = [ps("B_psN%d" % i, [128, 512], F32) for i in range(2)]
    psT = ps("B_psT", [128, 8, 128], BF16)
    S.dma("sp", cw[:], D.conv_w[:, :, :], [], ["cw"], "cw")
    S.add("pool", lambda e: e.memset(ones[:], 1.0), [], ["ones"])
    for fc in range(12):
        for j in range(5):
            S.add("dve", lambda e, fc=fc, j=j: e.tensor_scalar(
                out=Dg[:, fc, j, :], in0=C.ident_f[:], scalar1=cw[:, fc, j:j + 1], scalar2=None,
                op0=ALU.mult), ["cw", "ident_f"], ["Dg"])
    pq = D.PQKV.rearrange("(c p) n -> p c n", p=128)
    qkt_d = D.QKT.rearrange("a h d n -> d (a h) n")
    groups = [(g0, 512) for g0 in range(0, NE2 - 64, 512)] + [(NE2 - 64, 64)]
    def do_group(gi, g0, T):
        P_ = pin[gi % 2]
        pk = "pin%d" % (gi % 2)
        S.dma("sp", P_[:, :, 0:T + 4], pq[:, :, g0:g0 + T + 4], ["PQKV"], [pk], pk)
        for fc in range(12):
            pc = psC[fc % 2]
            pck = "psC%d" % (fc % 2)
            for j in range(5):
                S.add("pe", lambda e, fc=fc, j=j, pc=pc, P_=P_: e.matmul(
                    pc[:, 0:T], Dg[:, fc, j, :], P_[:, fc, j:j + T], start=(j == 0), stop=(j == 4)),
                    [pk, "Dg"], [pck])
            if fc < 8:
                S.add("act", lambda e, fc=fc, pc=pc: e.activation(out=ysb[:, fc, 0:T], in_=pc[:, 0:T],
                                                                  func=AF.Silu), [pck], ["ysb"])
                sq_ = sqb[fc % 2]
                sqk = "sqb%d" % (fc % 2)
                S.add("pool", lambda e, fc=fc, sq_=sq_: e.tensor_tensor(
                    out=sq_[:, 0:T], in0=ysb[:, fc, 0:T], in1=ysb[:, fc, 0:T], op=ALU.mult),
                    ["ysb"], [sqk])
                pn = psN[fc % 2]
                pnk = "psN%d" % (fc % 2)
                S.add("pe", lambda e, pn=pn, sq_=sq_: e.matmul(pn[:, 0:T], ones[:], sq_[:, 0:T],
                                                               start=True, stop=True), [sqk, "ones"], [pnk])
                S.add("dve", lambda e, fc=fc, pn=pn: e.tensor_scalar(
                    out=ssb[:, fc, 0:T], in0=pn[:, 0:T], scalar1=EPS, scalar2=None, op0=ALU.add),
                    [pnk], ["ssb"])
            else:
                S.add("act", lambda e, fc=fc, pc=pc: e.activation(out=kv_f[:, fc - 4, 0:T], in_=pc[:, 0:T],
                                                                  func=AF.Silu), [pck], ["kvf"])
        S.add("act", lambda e: e.activation(out=ssb[:, :, 0:T], in_=ssb[:, :, 0:T], func=AF.Ln),
              ["ssb"], ["ssb"])
        S.add("act", lambda e: e.activation(out=ssb[:, :, 0:T], in_=ssb[:, :, 0:T], func=AF.Exp, scale=-0.5),
              ["ssb"], ["ssb"])
        S.add("dve", lambda e: e.scalar_tensor_tensor(
            out=qk_o[:, 0:4, 0:T], in0=ysb[:, 0:4, 0:T], scalar=float(DH ** -0.5), in1=ssb[:, 0:4, 0:T],
            op0=ALU.mult, op1=ALU.mult), ["ysb", "ssb"], ["qko"])
        S.add("dve", lambda e: e.tensor_tensor(out=qk_o[:, 4:8, 0:T], in0=ysb[:, 4:8, 0:T],
                                               in1=ssb[:, 4:8, 0:T], op=ALU.mult), ["ysb", "ssb"], ["qko"])
        S.add("pool", lambda e: e.tensor_copy(kv_f[:, 0:4, 0:T], qk_o[:, 4:8, 0:T]), ["qko"], ["kvf"])
        S.dma("pool", qkt_d[:, :, g0:g0 + T], qk_o[:, :, 0:T], ["qko"], ["QKT"], "qko")
        for sbk in range(max(T // 128, 1)):
            do_sub(g0, T, sbk)

    def do_sub(g0, T, sbk):
        if True:
            P = min(T, 128)
            c0 = sbk * 128
            kt_ = kv_t[sbk % 2]
            ktk = "kvt%d" % (sbk % 2)
            for a in range(8):
                S.add("pe", lambda e, a=a, c0=c0, P=P: e.transpose(psT[0:P, a, :], kv_f[:, a, c0:c0 + P],
                                                                 C.ident_b[:, :]), ["kvf", "ident_b"], ["psT"])
            S.add("act", lambda e, kt_=kt_, P=P: e.activation(out=kt_[0:P, :, :], in_=psT[0:P, :, :],
                                                              func=AF.Copy), ["psT"], [ktk])
            S.dma("pool", D.KVT[g0 + c0:g0 + c0 + P, :, :, :].rearrange("n a h d -> n (a h) d"),
                  kt_[0:P, :, :], [ktk], ["KVT"], ktk)

    for gi, (g0, T) in enumerate(groups):
        do_group(gi, g0, T)


def phase_G(C, es):
    nc, S, D = C.nc, C.S, C.D
    NB = C.NB
    sb = lambda n, s, d=F32: _alloc(C, es, "sb", n, s, d)
    ps = lambda n, s, d=F32: _alloc(C, es, "ps", n, s, d)
    masks = sb("G_masks", [64, 12, 64], F32)
    ones64 = sb("G_ones64", [64, 128], F32)
    S.dma("sp", masks[:], D.masks[:, :, :], [], ["masks"], "masks")
    S.add("pool", lambda e: e.memset(ones64[:], 1.0), [], ["ones64"])
    I4 = masks[:, 8:12, :]
    idf = C.ident_f[0:64, 0:64]
    B = []
    for d in range(2):
        b = Ctx()
        n = lambda x: "G%d_%s" % (d, x)
        b.qk_in = [sb(n("qkin%d" % i), [128, 4, 2, 64], BF16) for i in range(2)]
        b.kv_in = [sb(n("kvin%d" % i), [64, 2, 4, 128], BF16) for i in range(2)]
        b.bgc = [sb(n("bgc%d" % i), [64, 16], F32) for i in range(2)]
        b.Gle = sb(n("Gle"), [64, 4, 64])
        b.egr = sb(n("egr"), [128, 4, 64])
        b.decT = sb(n("decT"), [64, 4, 64])
        b.dec = sb(n("dec"), [64, 4, 64])
        b.g3s = sb(n("g3s"), [128, 8])
        b.sc = sb(n("sc"), [64, 16])
        b.glast = sb(n("glast"), [128, 4])
        b.Nn = sb(n("Nn"), [64, 4, 64])
        b.NT = sb(n("NT"), [64, 4, 64])
        b.Pk = [sb(n("Pk%d" % i), [64, 4, 2, 64]) for i in range(2)]
        b.R = [sb(n("R%d" % i), [64, 4, 64]) for i in range(2)]
        b.qkTm = sb(n("qkTm"), [64, 4, 64], BF16)
        b.qdT = sb(n("qdT"), [128, 4, 64], BF16)
        b.vb = sb(n("vb"), [64, 4, 128])
        b.kbg = sb(n("kbg"), [64, 4, 128])
        b.kdec = sb(n("kdec"), [64, 4, 128], BF16)
        b.wT = sb(n("wT"), [128, 4, 64], BF16)
        b.u = sb(n("u"), [64, 4, 128])
        b.vnew = sb(n("vnew"), [64, 4, 128], BF16)
        b.osb = sb(n("osb"), [64, 4, 128])
        b.Sf = sb(n("Sf"), [128, 4, 128])
        b.Sb = sb(n("Sb"), [128, 4, 128], BF16)
        b.pb = [ps(n("pb%d" % i), [128, 512], F32) for i in range(4)]
        S.add("pool", lambda e, b=b: e.memset(b.Sf[:], 0.0), [], [n("Sf")])
        S.add("pool", lambda e, b=b: e.memset(b.Sb[:], 0.0), [], [n("Sb")])
        B.append(b)
    qkt_k = D.QKT[1, :, :, :].rearrange("h d n -> d h n")
    qkt_q = D.QKT[0, :, :, :].rearrange("h d n -> d h n")

    def unit(d, s_, j, typ):
        b = B[d]
        r = lambda x: "G%d_%s" % (d, x)
        M1, M2, NQ, NA = (0, 1, 4, 6) if typ == "F" else (2, 3, 5, 7)
        bcol = 0 if d == 0 else 4
        gcol = 8 if d == 0 else 12
        n0 = 64 * j
        par = s_ % 2
        qk_in, kv_in, bgc = b.qk_in[par], b.kv_in[par], b.bgc[par]
        rq, rk, rb = r("qkin%d" % par), r("kvin%d" % par), r("bgc%d" % par)
        b0, b1, b2, b3 = b.pb
        p0, p1, p2, p3 = r("pb0"), r("pb1"), r("pb2"), r("pb3")
        S.dma("sp", qk_in[:, :, 0, :], qkt_k[:, :, n0:n0 + 64], ["QKT"], [rq], rq)
        S.dma("sp", qk_in[:, :, 1, :], qkt_q[:, :, n0:n0 + 64], ["QKT"], [rq], rq)
        S.dma("sp", kv_in[:], D.KVT[n0:n0 + 64, :, :, :], ["KVT"], [rk], rk)
        S.dma("sp", bgc[:], D.BG[n0:n0 + 64, :], ["BG"], [rb], rb)
        yield
        g4 = bgc[:, gcol:gcol + 4]
        be4 = bgc[:, bcol:bcol + 4]
        S.add("dve", lambda e: e.tensor_tensor(
            out=b.Gle[:], in0=masks[:, M1, :].unsqueeze(1).broadcast_to([64, 4, 64]),
            in1=g4.unsqueeze(2).broadcast_to([64, 4, 64]), op=ALU.mult), [rb, "masks"], [r("Gle")])
        S.add("dve", lambda e: e.tensor_scalar(out=b.sc[:, 8:12], in0=be4, scalar1=-1.0, scalar2=None,
                                               op0=ALU.mult), [rb], [r("sc_nb")])
        yield
        Gf = b.Gle[:].rearrange("p h c -> p (h c)")
        I4f = I4.rearrange("p h c -> p (h c)")
        S.add("pe", lambda e: e.matmul(b0[:, 0:256], ones64[:, :], Gf, start=True, stop=True),
              [r("Gle"), "ones64"], [p0])
        S.add("pe", lambda e: e.matmul(b0[0:64, 256:512], masks[:, NQ, :], I4f, start=True, stop=False),
              ["masks"], [p0])
        S.add("pe", lambda e: e.matmul(b0[0:64, 256:512], masks[:, M2, :], Gf, start=False, stop=True),
              [r("Gle"), "masks"], [p0])
        for h in range(4):
            S.add("pe", lambda e, h=h: e.matmul(b1[0:64, h * 64:(h + 1) * 64], masks[:, NA, :], masks[:, 8, :],
                                                start=True, stop=False), ["masks"], [p1])
            S.add("pe", lambda e, h=h: e.matmul(b1[0:64, h * 64:(h + 1) * 64], b.Gle[:, h, :], masks[:, M2, :],
                                                start=False, stop=True), [r("Gle"), "masks"], [p1])
        S.add("pe", lambda e: e.matmul(b1[:, 256:260], ones64[:, :], g4, start=True, stop=True),
              [rb, "ones64"], [p1])
        S.add("pe", lambda e: e.matmul(b1[0:64, 260:264], masks[:, M1, :], g4, start=True, stop=True),
              [rb, "masks"], [p1])
        yield
        _skip = set(int(x) for x in os.environ.get("GSKIP", "").split(",") if x)
        _c4 = [0]

        def A4(*a_, **k_):
            i_ = _c4[0]
            _c4[0] += 1
            if i_ in _skip:
                return
            S.add(*a_, **k_)
        A4("act", lambda e: e.activation(out=b.egr[:].rearrange("p h c -> p (h c)"), in_=b0[:, 0:256],
                                            func=AF.Exp), [p0], [r("egr")])
        A4("act", lambda e: e.activation(out=b.decT[:].rearrange("p h c -> p (h c)"), in_=b0[0:64, 256:512],
                                            func=AF.Exp), [p0], [r("decT")])
        A4("dve", lambda e: e.tensor_copy(b.dec[:].rearrange("p h c -> p (h c)"), b1[0:64, 0:256]),
           [p1], [r("dec")])
        if os.environ.get("GDUMP", "0") == "1":
            S.dma("pool", D.OF[d, 0:64, 0:256], b.dec[:].rearrange("p h c -> p (h c)"), [r("dec")], ["OF"], r("dd"))
        if os.environ.get("GEXP", "1") == "1":
            S.add("act", lambda e: e.activation(out=b.dec[:].rearrange("p h c -> p (h c)"),
                                                in_=b.dec[:].rearrange("p h c -> p (h c)"), func=AF.Exp),
                  [r("dec")], [r("dec")])
        A4("dve", lambda e: e.tensor_copy(b.g3s[:, 0:4], b1[:, 256:260]), [p1], [r("g3s")])
        A4("dve", lambda e: e.tensor_copy(b.g3s[0:64, 4:8], b1[0:64, 260:264]), [p1], [r("g3s")])
        A4("act", lambda e: e.activation(out=b.sc[:, 0:4], in_=b.g3s[0:64, 4:8], func=AF.Exp),
              [r("g3s")], [r("sc_eg")])
        A4("act", lambda e: e.activation(out=b.glast[:], in_=b.g3s[:, 0:4], func=AF.Exp),
              [r("g3s")], [r("glast")])
        A4("dve", lambda e: e.tensor_tensor(out=b.sc[:, 4:8], in0=b.g3s[0:64, 0:4], in1=b.g3s[0:64, 4:8],
                                               op=ALU.subtract), [r("g3s")], [r("sc_kd")])
        A4("act", lambda e: e.activation(out=b.sc[:, 4:8], in_=b.sc[:, 4:8], func=AF.Exp),
              [r("sc_kd")], [r("sc_kd")])
        A4("dve", lambda e: e.tensor_tensor(out=b.sc[:, 12:16], in0=be4, in1=b.sc[:, 0:4], op=ALU.mult),
              [rb, r("sc_eg")], [r("sc_bg")])
        yield
        bc = lambda ap: ap.unsqueeze(2).broadcast_to([64, 4, 128])
        S.add("pool", lambda e: e.tensor_tensor(out=b.vb[:], in0=kv_in[:, 1, :, :], in1=bc(be4), op=ALU.mult),
              [rk, rb], [r("vb")])
        S.add("pool", lambda e: e.tensor_tensor(out=b.kbg[:], in0=kv_in[:, 0, :, :], in1=bc(b.sc[:, 12:16]),
                                                op=ALU.mult), [rk, r("sc_bg")], [r("kbg")])
        S.add("pool", lambda e: e.tensor_tensor(out=b.kdec[:], in0=kv_in[:, 0, :, :], in1=bc(b.sc[:, 4:8]),
                                                op=ALU.mult), [rk, r("sc_kd")], [r("kdec")])
        for h in range(4):
            S.add("pe", lambda e, h=h: e.matmul(b2[0:64, h * 128:(h + 1) * 128], qk_in[:, h, 0, :],
                                                qk_in[:, h, :, :].rearrange("p a c -> p (a c)"),
                                                start=True, stop=True), [rq], [p2])
        yield
        kq = b2[0:64, :].rearrange("p (h a c) -> p h a c", h=4, a=2)
        for h in range(4):
            S.add("dve", lambda e, h=h: e.scalar_tensor_tensor(
                out=b.Nn[:, h, :], in0=kq[:, h, 0, :], scalar=b.sc[:, 8 + h:9 + h], in1=b.dec[:, h, :],
                op0=ALU.mult, op1=ALU.mult), [p2, r("sc_nb"), r("dec")], [r("Nn")])
        S.add("dve", lambda e: e.tensor_tensor(out=b.qkTm[:], in0=kq[:, :, 1, :], in1=b.decT[:], op=ALU.mult),
              [p2, r("decT")], [r("qkTm")])
        S.add("pool", lambda e: e.tensor_tensor(out=b.qdT[:], in0=qk_in[:, :, 1, :], in1=b.egr[:], op=ALU.mult),
              [rq, r("egr")], [r("qdT")])
        yield
        for h in range(4):
            S.add("pe", lambda e, h=h: e.transpose(b3[0:64, h * 64:(h + 1) * 64], b.Nn[:, h, :], idf),
                  [r("Nn"), "ident_f"], [p3])
        yield
        S.add("dve", lambda e: e.tensor_copy(b.NT[:].rearrange("p h c -> p (h c)"), b3[0:64, 0:256]),
              [p3], [r("NT")])
        S.add("dve", lambda e: e.tensor_tensor(out=b.R[0][:].rearrange("p h c -> p (h c)"), in0=b3[0:64, 0:256],
                                               in1=I4f, op=ALU.add), [p3, "masks"], [r("R0")])
        yield
        for k in range(1, 6):
            cur, prv = k % 2, (k - 1) % 2
            bankP, pk = (b0, p0) if k % 2 == 1 else (b1, p1)
            bankR, prk = (b2, p2) if k % 2 == 1 else (b3, p3)
            Pcur = b.Pk[cur]
            if k == 1:
                Pp = lambda h: b.NT[:, h, :]
                PTp = lambda h: b.Nn[:, h, :]
                rp = [r("NT"), r("Nn")]
            else:
                Pp = lambda h, prv=prv: b.Pk[prv][:, h, 0, :]
                PTp = lambda h, prv=prv: b.Pk[prv][:, h, 1, :]
                rp = [r("Pk%d" % prv)]
            for h in range(4):
                if k < 5:
                    S.add("pe", lambda e, h=h, Pp=Pp, PTp=PTp, bankP=bankP: e.matmul(
                        bankP[0:64, h * 128:h * 128 + 64], PTp(h), Pp(h), start=True, stop=True), rp, [pk])
                S.add("pe", lambda e, h=h, Pp=Pp, PTp=PTp, bankP=bankP: e.matmul(
                    bankP[0:64, h * 128 + 64:h * 128 + 128], Pp(h), PTp(h), start=True, stop=True), rp, [pk])
            yield
            pv = bankP[0:64, :].rearrange("p (h a c) -> p h a c", h=4, a=2)
            if k < 5:
                S.add("dve", lambda e, Pcur=Pcur, bankP=bankP: e.tensor_copy(
                    Pcur[:].rearrange("p h a c -> p (h a c)"), bankP[0:64, :]),
                    [pk], [r("Pk%d" % cur)])
            else:
                S.add("dve", lambda e, Pcur=Pcur, pv=pv: e.tensor_copy(Pcur[:, :, 1, :], pv[:, :, 1, :]),
                      [pk], [r("Pk%d" % cur)])
            yield
            for h in range(4):
                S.add("pe", lambda e, h=h, Pcur=Pcur, bankR=bankR, prv=prv: e.matmul(
                    bankR[0:64, h * 64:(h + 1) * 64], Pcur[:, h, 1, :], b.R[prv][:, h, :], start=True, stop=True),
                    [r("Pk%d" % cur), r("R%d" % prv)], [prk])
            yield
            S.add("dve", lambda e, cur=cur, prv=prv, bankR=bankR: e.tensor_tensor(
                out=b.R[cur][:].rearrange("p h c -> p (h c)"), in0=bankR[0:64, 0:256],
                in1=b.R[prv][:].rearrange("p h c -> p (h c)"), op=ALU.add), [prk, r("R%d" % prv)], [r("R%d" % cur)])
            yield
        X = b.R[1]
        rX = r("R1")
        for h in range(4):
            S.add("pe", lambda e, h=h: e.matmul(b1[:, h * 64:(h + 1) * 64], b.kbg[:, h, :], X[:, h, :],
                                                start=True, stop=True), [r("kbg"), rX], [p1])
        for h in range(4):
            S.add("pe", lambda e, h=h: e.matmul(b3[0:64, h * 128:(h + 1) * 128], X[:, h, :], b.vb[:, h, :],
                                                start=True, stop=True), [r("vb"), rX], [p3])
        yield
        S.add("dve", lambda e: e.tensor_copy(b.wT[:].rearrange("p h c -> p (h c)"), b1[:, 0:256]),
              [p1], [r("wT")])
        S.add("dve", lambda e: e.tensor_copy(b.u[:].rearrange("p h c -> p (h c)"), b3[0:64, :]), [p3], [r("u")])
        yield
        for h in range(4):
            S.add("pe", lambda e, h=h: e.matmul(b0[0:64, h * 128:(h + 1) * 128], b.wT[:, h, :], b.Sb[:, h, :],
                                                start=True, stop=True), [r("wT"), r("Sb")], [p0])
        yield
        S.add("dve", lambda e: e.tensor_tensor(out=b.vnew[:].rearrange("p h c -> p (h c)"),
                                               in0=b.u[:].rearrange("p h c -> p (h c)"), in1=b0[0:64, :],
                                               op=ALU.subtract), [r("u"), p0], [r("vnew")])
        yield
        for h in range(4):
            S.add("pe", lambda e, h=h: e.matmul(b2[0:64, h * 128:(h + 1) * 128], b.qdT[:, h, :], b.Sb[:, h, :],
                                                start=True, stop=False), [r("qdT"), r("Sb")], [p2])
            S.add("pe", lambda e, h=h: e.matmul(b2[0:64, h * 128:(h + 1) * 128], b.qkTm[:, h, :], b.vnew[:, h, :],
                                                start=False, stop=True), [r("qkTm"), r("vnew")], [p2])
        for h in range(4):
            S.add("pe", lambda e, h=h: e.matmul(b1[:, h * 128:(h + 1) * 128], b.kdec[:, h, :], b.vnew[:, h, :],
                                                start=True, stop=True), [r("kdec"), r("vnew")], [p1])
        yield
        S.add("dve", lambda e: e.tensor_copy(b.osb[:].rearrange("p h c -> p (h c)"), b2[0:64, :]),
              [p2], [r("osb")])
        S.dma("pool", D.OF[d, n0:n0 + 64, :], b.osb[:].rearrange("p h c -> p (h c)"), [r("osb")], ["OF"], r("osb"))
        for h in range(4):
            S.add("dve", lambda e, h=h: e.scalar_tensor_tensor(
                out=b.Sf[:, h, :], in0=b.Sf[:, h, :], scalar=b.glast[:, h:h + 1], in1=b1[:, h * 128:(h + 1) * 128],
                op0=ALU.mult, op1=ALU.add), [r("Sf"), r("glast"), p1], [r("Sf")])
        S.add("act", lambda e: e.activation(out=b.Sb[:].rearrange("p h c -> p (h c)"),
                                            in_=b.Sf[:].rearrange("p h c -> p (h c)"), func=AF.Copy),
              [r("Sf")], [r("Sb")])
        yield

    gstop = int(os.environ.get("GSTOP", "1000"))
    gnb = int(os.environ.get("GNB", str(NB)))
    for s_ in range(min(NB, gnb)):
        jb = 0 if s_ == 0 else NB - s_
        gens = [unit(0, s_, s_, "F"), unit(1, s_, jb, "F" if s_ == 0 else "R")]
        alive = [True, True]
        cnt = 0
        while any(alive) and cnt < gstop:
            cnt += 1
            for gi, g in enumerate(gens):
                if alive[gi]:
                    try:
                        next(g)
                    except StopIteration:
                        alive[gi] = False


def phase_C(C, es):
    nc, S, D = C.nc, C.S, C.D
    sb = lambda n, s, d=F32: _alloc(C, es, "sb", n, s, d)
    ps = lambda n, s, d=F32: _alloc(C, es, "ps", n, s, d)
    NKT = C.NKT
    qtb = [sb("C_qtb%d" % i, [64, 512], BF16) for i in range(2)]
    pT = [sb("C_pT%d" % i, [128, 512], BF16) for i in range(3)]
    rinv = sb("C_rinv", [128, 512], F32)
    oT = [sb("C_oT%d" % i, [64, 512], BF16) for i in range(2)]
    psS = [ps("C_psS%d" % i, [128, 512], F32) for i in range(3)]
    psO = [ps("C_psO%d" % i, [128, 512], F32) for i in range(2)]

    def block(it, qb, kvh):
        q_ = qtb[it % 2]
        qk = "qtb%d" % (it % 2)
        po = psO[it % 2]
        pok = "psO%d" % (it % 2)
        o_ = oT[it % 2]
        ok = "oT%d" % (it % 2)
        S.dma("sp", q_[:], D.QT[kvh, :, qb, :, :].rearrange("d g q -> d (g q)"), ["QT"], [qk], qk)
        for kt in range(NKT):
            nk = N_META if kt == 0 else 128
            kc0 = 0 if kt == 0 else N_META + (kt - 1) * 128
            i3 = (it * NKT + kt) % 3
            pS, pSk = psS[i3], "psS%d" % i3
            p_, pk = pT[i3], "pT%d" % i3
            S.add("pe", lambda e, nk=nk, kc0=kc0, pS=pS: e.matmul(
                pS[0:nk, :], C.KT[:, kvh, kc0:kc0 + nk], q_[:], start=True, stop=True), ["KT", qk], [pSk])
            S.add("act", lambda e, nk=nk, pS=pS, p_=p_: e.activation(out=p_[0:nk, :], in_=pS[0:nk, :],
                                                                    func=AF.Exp, scale=0.125), [pSk], [pk])
            S.add("pe", lambda e, nk=nk, kt=kt, p_=p_: e.matmul(
                po[:, :], C.VA[0:nk, kt, kvh, :], p_[0:nk, :], start=(kt == 0), stop=(kt == NKT - 1)),
                ["VA", pk], [pok])
        S.add("dve", lambda e: e.reciprocal(out=rinv[64:128, :], in_=po[64:128, :]), [pok], ["rinv"])
        S.add("dve", lambda e: e.tensor_tensor(out=o_[:], in0=po[0:64, :], in1=rinv[64:128, :], op=ALU.mult),
              [pok, "rinv"], [ok])
        S.dma("pool", D.OAT[kvh, :, qb, :, :].rearrange("d g q -> d (g q)"), o_[:], [ok], ["OAT"], ok)

    it = 0
    for qb in range(C.NQB):
        for kvh in range(2):
            block(it, qb, kvh)
            it += 1


def phase_D1(C, es):
    nc, S, D = C.nc, C.S, C.D
    SEQ = C.SEQ
    sb = lambda n, s, d=F32: _alloc(C, es, "sb", n, s, d)
    ps = lambda n, s, d=F32: _alloc(C, es, "ps", n, s, d)
    Wdn = sb("D_Wdn", [128, 4, D_MODEL], BF16)
    Wat = sb("D_Wat", [64, 8, D_MODEL], BF16)
    wst = sb("D_wst", [128, 4, D_MODEL], F32)
    dnw = sb("D_dnw", [128, 512], F32)
    npost = sb("D_npost", [128, D_MODEL], F32)
    of_t = [sb("D_of%d" % i, [128, 512], F32) for i in range(2)]
    ob_t = [sb("D_ob%d" % i, [128, 512], F32) for i in range(2)]
    zs_t = [sb("D_zs%d" % i, [128, 512], F32) for i in range(2)]
    xt = [sb("D_xt%d" % i, [128, D_MODEL], F32) for i in range(2)]
    oat = [sb("D_oat%d" % i, [64, 8, 128], BF16) for i in range(2)]
    osum = sb("D_osum", [128, 512], F32)
    sq = sb("D_sq", [128, 512], F32)
    ss4 = sb("D_ss4", [128, 4], F32)
    on = sb("D_on", [128, 512], F32)
    onb = sb("D_onb", [128, 512], BF16)
    odT = sb("D_odT", [128, 4, 128], BF16)
    junk = sb("D_junk", [128, 512], F32)
    ssm = sb("D_ssm", [128, 4], F32)
    tmix = sb("D_tmix", [128, D_MODEL], F32)
    h1 = [sb("D_h1%d" % i, [128, D_MODEL], F32) for i in range(2)]
    psT = ps("D_psT", [128, 4, 128], BF16)
    psM = [ps("D_psM%d" % i, [128, 512], F32) for i in range(4)]
    S.dma("sp", dnw[:], D.dn_norm[:, :], [], ["dnw"], "dnw")
    S.dma("sp", npost[:], D.nmix_post[:, :], [], ["npost"], "npost")
    S.dma("sp", wst[:], D.w_out_dn[:, :, :], [], ["wst"], "wst")
    S.add("dve", lambda e: e.tensor_copy(Wdn[:], wst[:]), ["wst"], ["Wdn"])
    for hh in range(2):
        S.dma("sp", wst[0:64, :, :], D.w_out_at[:, hh * 4:(hh + 1) * 4, :], [], ["wst"], "wst")
        S.add("dve", lambda e, hh=hh: e.tensor_copy(Wat[:, hh * 4:(hh + 1) * 4, :], wst[0:64, :, :]),
              ["wst"], ["Wat"])

    def tile(ti):
        r0 = ti * 128
        e0 = 64 + r0
        p = ti % 2
        k = lambda x: "%s%d" % (x, p)
        S.dma("sp", of_t[p][:], D.OF[0, e0:e0 + 128, :], ["OF"], [k("of")], k("of"))
        S.dma("sp", ob_t[p][:], D.OF[1, e0:e0 + 128, :], ["OF"], [k("ob")], k("ob"))
        S.dma("sp", zs_t[p][:], D.ZS[r0:r0 + 128, :], ["ZS"], [k("zs")], k("zs"))
        S.dma("sp", xt[p][:], D.x[r0:r0 + 128, :], [], [k("xt")], k("xt"))
        S.dma("sp", oat[p][:].rearrange("d (k g) q -> d k g q", k=2),
              D.OAT[:, :, ti, :, :].rearrange("k d g q -> d k g q"), ["OAT"], [k("oat")], k("oat"))
        S.add("pool", lambda e: e.tensor_tensor(out=osum[:], in0=of_t[p][:], in1=ob_t[p][:], op=ALU.add),
              [k("of"), k("ob")], ["osum"])
        S.add("act", lambda e: e.activation(out=sq[:], in_=osum[:], func=AF.Square), ["osum"], ["sq"])
        S.add("dve", lambda e: e.tensor_reduce(out=ss4[:], in_=sq[:].rearrange("p (h d) -> p h d", d=128),
                                               op=ALU.add, axis=AX.X), ["sq"], ["ss4"])
        S.add("act", lambda e: e.activation(out=ss4[:], in_=ss4[:], func=AF.Sqrt, scale=1.0 / 128,
                                            bias=C.eps_t[:, :]), ["ss4"], ["ss4"])
        S.add("dve", lambda e: e.reciprocal(out=ss4[:], in_=ss4[:]), ["ss4"], ["ss4"])
        S.add("dve", lambda e: e.tensor_tensor(
            out=on[:].rearrange("p (h d) -> p h d", d=128), in0=osum[:].rearrange("p (h d) -> p h d", d=128),
            in1=ss4[:].unsqueeze(2).broadcast_to([128, 4, 128]), op=ALU.mult), ["osum", "ss4"], ["on"])
        S.add("pool", lambda e: e.tensor_tensor(out=on[:], in0=on[:], in1=dnw[:], op=ALU.mult),
              ["on", "dnw"], ["on"])
        S.add("dve", lambda e: e.tensor_tensor(out=onb[:], in0=on[:], in1=zs_t[p][:], op=ALU.mult),
              ["on", k("zs")], ["onb"])
        for c in range(4):
            S.add("pe", lambda e, c=c: e.transpose(psT[:, c, :], onb[:, c * 128:(c + 1) * 128], C.ident_b[:, :]),
                  ["onb", "ident_b"], ["psT"])
        S.add("act", lambda e: e.activation(out=odT[:], in_=psT[:], func=AF.Copy), ["psT"], ["odT"])
        for half in range(2):
            pm = psM[p * 2 + half]
            pmk = "psM%d" % (p * 2 + half)
            cs = slice(half * 512, (half + 1) * 512)
            for c in range(4):
                S.add("pe", lambda e, c=c, pm=pm, cs=cs: e.matmul(pm[:, :], odT[:, c, :], Wdn[:, c, cs],
                                                                  start=(c == 0), stop=False),
                      ["odT", "Wdn"], [pmk])
            for h in range(8):
                S.add("pe", lambda e, h=h, pm=pm, cs=cs: e.matmul(pm[:, :], oat[p][:, h, :], Wat[:, h, cs],
                                                                  start=False, stop=(h == 7)),
                      [k("oat"), "Wat"], [pmk])
            S.add("act", lambda e, pm=pm, half=half: e.activation(out=junk[:], in_=pm[:, :], func=AF.Square,
                                                                 accum_out=ssm[:, half:half + 1]),
                  [pmk], ["junk", "ssm%d" % half])
        S.add("dve", lambda e: e.tensor_tensor(out=ssm[:, 2:3], in0=ssm[:, 0:1], in1=ssm[:, 1:2], op=ALU.add),
              ["ssm0", "ssm1"], ["ssm2"])
        S.add("act", lambda e: e.activation(out=ssm[:, 2:3], in_=ssm[:, 2:3], func=AF.Sqrt, scale=1.0 / D_MODEL,
                                            bias=C.eps_t[:, :]), ["ssm2"], ["ssm2"])
        S.add("dve", lambda e: e.reciprocal(out=ssm[:, 3:4], in_=ssm[:, 2:3]), ["ssm2"], ["ssm3"])
        for half in range(2):
            pm = psM[p * 2 + half]
            pmk = "psM%d" % (p * 2 + half)
            cs = slice(half * 512, (half + 1) * 512)
            S.add("dve", lambda e, pm=pm, cs=cs: e.scalar_tensor_tensor(
                out=tmix[:, cs], in0=pm[:, :], scalar=ssm[:, 3:4], in1=npost[:, cs], op0=ALU.mult, op1=ALU.mult),
                [pmk, "ssm3", "npost"], ["tmix"])
        S.add("pool", lambda e: e.tensor_tensor(out=h1[p][:], in0=tmix[:], in1=xt[p][:], op=ALU.add),
              ["tmix", k("xt")], [k("h1")])
        S.dma("pool", D.H1[r0:r0 + 128, :], h1[p][:], [k("h1")], ["H1"], k("h1"))

    for ti in range(SEQ // 128):
        tile(ti)


def phase_D2(C, es):
    nc, S, D = C.nc, C.S, C.D
    SEQ = C.SEQ
    sb = lambda n, s, d=F32: _alloc(C, es, "sb", n, s, d)
    ps = lambda n, s, d=F32: _alloc(C, es, "ps", n, s, d)
    Wup = sb("E_Wup", [128, 8, D_FF], BF16)
    Wdn = sb("E_Wdn", [128, 32, D_MODEL], BF16)
    npre = sb("E_npre", [128, 8], F32)
    npost = sb("E_npost", [128, D_MODEL], F32)
    with ExitStack() as ses:
        wst = _alloc(C, ses, "sb", "E_wst", [128, 4096], F32)
        S.dma("sp", npre[:], D.nmlp_pre[:, :], [], ["npre"], "npre")
        S.dma("sp", npost[:], D.nmlp_post[:, :], [], ["npost"], "npost")
        for kc in range(8):
            S.dma("sp", wst[:], D.w_up[:, kc, :], [], ["wst"], "wst")
            S.add("dve" if kc % 2 == 0 else "pool", lambda e, kc=kc: e.tensor_scalar(
                out=Wup[:, kc, :], in0=wst[:], scalar1=npre[:, kc:kc + 1], scalar2=None, op0=ALU.mult),
                ["wst", "npre"], ["Wup"])
        for c4 in range(8):
            S.dma("sp", wst[:].rearrange("p (c n) -> p c n", c=4), D.w_down[:, c4 * 4:(c4 + 1) * 4, :], [], ["wst"],
                  "wst")
            S.add("act", lambda e, c4=c4: e.activation(
                out=Wdn[:, c4 * 4:(c4 + 1) * 4, :].rearrange("p c n -> p (c n)"), in_=wst[:], func=AF.Copy),
                ["wst"], ["Wdn"])
        S.phase_end()
    h1 = [sb("E_h1%d" % i, [128, D_MODEL], F32) for i in range(4)]
    hb = sb("E_hb", [128, D_MODEL], BF16)
    hT = sb("E_hT", [128, 8, 512], BF16)
    hid = sb("E_hid", [128, 32, 512], BF16)
    rl = [sb("E_rl%d" % i, [128, 512], F32) for i in range(2)]
    junk = sb("E_junk", [128, D_MODEL], BF16)
    st = sb("E_st", [128, 4, 8], F32)
    tf = sb("E_tf", [128, D_MODEL], F32)
    ot = [sb("E_ot0", [128, D_MODEL], F32)] * 2
    pT = ps("E_pT", [128, 8, 128], BF16)
    psU = [ps("E_psU%d" % i, [128, 512], F32) for i in range(2)]
    psD = [ps("E_psD%d" % i, [128, 512], F32) for i in range(4)]

    def pre_tile(st_i, i):
        r0 = st_i * 512 + i * 128
        hk = "h1_%d" % i
        S.dma("sp", h1[i][:], D.H1[r0:r0 + 128, :], ["H1"], [hk], hk)
        S.add("act", lambda e: e.activation(out=junk[:], in_=h1[i][:], func=AF.Square,
                                            accum_out=st[:, i, 0:1]), [hk], ["junk", "st%d" % i])
        S.add("dve", lambda e: e.tensor_scalar(out=st[:, i, 1:2], in0=st[:, i, 0:1], scalar1=1.0 / D_MODEL,
                                               scalar2=EPS, op0=ALU.mult, op1=ALU.add), ["st%d" % i], ["st%d" % i])
        S.add("dve", lambda e: e.reciprocal(out=st[:, i, 1:2], in_=st[:, i, 1:2]), ["st%d" % i], ["st%d" % i])
        S.add("dve", lambda e: e.tensor_tensor(out=st[:, i, 2:3], in0=st[:, i, 1:2], in1=st[:, i, 1:2],
                                               op=ALU.mult), ["st%d" % i], ["st%d" % i])
        S.add("pool", lambda e: e.tensor_copy(hb[:], h1[i][:]), [hk], ["hb"])
        for kc in range(8):
            S.add("pe", lambda e, kc=kc: e.transpose(pT[:, kc, :], hb[:, kc * 128:(kc + 1) * 128], C.ident_b[:, :]),
                  ["hb", "ident_b"], ["pT"])
        S.add("dve", lambda e: e.tensor_copy(hT[:, :, i * 128:(i + 1) * 128], pT[:]), ["pT"], ["hT"])

    def up(fc):
        pu, puk = psU[fc % 2], "psU%d" % (fc % 2)
        r_, rk = rl[fc % 2], "rl%d" % (fc % 2)
        for kc in range(8):
            S.add("pe", lambda e, kc=kc: e.matmul(pu[:, :], Wup[:, kc, fc * 128:(fc + 1) * 128], hT[:, kc, :],
                                                  start=(kc == 0), stop=(kc == 7)), ["Wup", "hT"], [puk])
        S.add("act", lambda e: e.activation(out=r_[:], in_=pu[:, :], func=AF.Relu), [puk], [rk])
        S.add("pool" if fc % 2 == 0 else "dve",
              lambda e: e.tensor_tensor(out=hid[:, fc, :], in0=r_[:], in1=r_[:], op=ALU.mult), [rk], ["hid"])

    def down_tile(st_i, i):
        r0 = st_i * 512 + i * 128
        hk = "h1_%d" % i
        sk = "st%d" % i
        for half in range(2):
            pd, pdk = psD[(i % 2) * 2 + half], "psD%d" % ((i % 2) * 2 + half)
            cs = slice(half * 512, (half + 1) * 512)
            for fc in range(32):
                S.add("pe", lambda e, fc=fc, pd=pd, cs=cs: e.matmul(
                    pd[:, :], hid[:, fc, i * 128:(i + 1) * 128], Wdn[:, fc, cs], start=(fc == 0), stop=(fc == 31)),
                    ["hid", "Wdn"], [pdk])
            S.add("act", lambda e, pd=pd, half=half: e.activation(out=junk[:, 0:512], in_=pd[:, :], func=AF.Square,
                                                                 accum_out=st[:, i, 3 + half:4 + half]),
                  [pdk], ["junk", sk])
        S.add("dve", lambda e: e.tensor_tensor(out=st[:, i, 5:6], in0=st[:, i, 3:4], in1=st[:, i, 4:5], op=ALU.add),
              [sk], [sk])
        S.add("dve", lambda e: e.tensor_tensor(out=st[:, i, 5:6], in0=st[:, i, 5:6], in1=st[:, i, 2:3], op=ALU.mult),
              [sk], [sk])
        S.add("act", lambda e: e.activation(out=st[:, i, 5:6], in_=st[:, i, 5:6], func=AF.Sqrt, scale=1.0 / D_MODEL,
                                            bias=C.eps_t[:, :]), [sk], [sk])
        S.add("dve", lambda e: e.reciprocal(out=st[:, i, 6:7], in_=st[:, i, 5:6]), [sk], [sk])
        S.add("dve", lambda e: e.tensor_tensor(out=st[:, i, 7:8], in0=st[:, i, 6:7], in1=st[:, i, 1:2], op=ALU.mult),
              [sk], [sk])
        o_ = ot[0]
        okk = "ot0"
        for half in range(2):
            pd, pdk = psD[(i % 2) * 2 + half], "psD%d" % ((i % 2) * 2 + half)
            cs = slice(half * 512, (half + 1) * 512)
            S.add("dve", lambda e, pd=pd, cs=cs: e.scalar_tensor_tensor(
                out=tf[:, cs], in0=pd[:, :], scalar=st[:, i, 7:8], in1=npost[:, cs], op0=ALU.mult, op1=ALU.mult),
                [pdk, sk, "npost"], ["tf"])
        S.add("pool", lambda e: e.tensor_tensor(out=o_[:], in0=tf[:], in1=h1[i][:], op=ALU.add), ["tf", hk], [okk])
        S.dma("pool", D.out[r0:r0 + 128, :], o_[:], [okk], ["out"], okk)

    for st_i in range(SEQ // 512):
        for i in range(4):
            pre_tile(st_i, i)
        for fc in range(32):
            up(fc)
        for i in range(4):
            down_tile(st_i, i)


def _consts(SEQ):
    r = (np.arange(SEQ) // 64).astype(np.float64)
    c = (np.arange(SEQ) % 64).astype(np.float64)
    F = 16
    freqs = (10000.0 ** (-np.arange(F, dtype=np.float32) / F)).astype(np.float32)
    ang = np.concatenate([r[:, None].astype(np.float32) * freqs, c[:, None].astype(np.float32) * freqs],
                         axis=-1).astype(np.float32)
    cos = np.cos(ang.astype(np.float64)).astype(np.float32)
    sin = np.sin(ang.astype(np.float64)).astype(np.float32)
    cos8 = np.ascontiguousarray(np.tile(cos, (1, 8)))
    sin8 = np.ascontiguousarray(np.tile(sin, (1, 8)))
    i = np.arange(64)
    BIG = 30000.0
    mm_, xx_ = i[:, None], i[None, :]
    m = np.zeros((64, 12, 64), np.float32)
    m[:, 0, :] = (mm_ <= xx_)
    m[:, 1, :] = (mm_ > xx_)
    m[:, 2, :] = (mm_ >= xx_)
    m[:, 3, :] = (mm_ < xx_)
    m[:, 4, :] = -BIG * (mm_ < xx_)
    m[:, 5, :] = -BIG * (mm_ > xx_)
    m[:, 6, :] = -BIG * (xx_ <= mm_)
    m[:, 7, :] = -BIG * (xx_ >= mm_)
    for h in range(4):
        m[:, 8 + h, :] = np.eye(64)
    return {"ident": np.eye(128, dtype=np.float32), "rope_cos": cos8, "rope_sin": sin8, "masks": m}


def prep_shared(inp, SEQ):
    f = lambda a: np.ascontiguousarray(a, dtype=np.float32)
    bc = lambda v, n=128: f(np.broadcast_to(np.asarray(v).reshape(1, -1), (n, np.asarray(v).size)))
    d = {}
    d["meta"] = f(inp["meta_tokens"])
    d["w_in"] = f(inp["w_in"][0].reshape(8, 128, IN_COLS).transpose(1, 0, 2))
    d["nmix_pre"] = f(inp["norm_mix_pre"][0].reshape(8, 128).T)
    d["conv_w"] = f(inp["conv_w"][0].reshape(5, 12, 128).transpose(2, 1, 0))
    d["a_log"] = bc(inp["a_log"][0].reshape(-1))
    d["dt_bias"] = bc(inp["dt_bias"][0].reshape(-1))
    d["dn_norm"] = bc(np.tile(inp["dn_out_norm"][0], 4))
    d["q_norm"] = bc(np.tile(inp["q_norm"][0], 8))
    d["k_norm"] = bc(np.tile(inp["k_norm"][0], 2))
    wo = inp["w_out"][0]
    d["w_out_dn"] = f(wo[0:512].reshape(4, 128, D_MODEL).transpose(1, 0, 2))
    d["w_out_at"] = f(wo[512:1024].reshape(8, 64, D_MODEL).transpose(1, 0, 2))
    d["nmix_post"] = bc(inp["norm_mix_post"][0])
    d["nmlp_pre"] = f(inp["norm_mlp_pre"][0].reshape(8, 128).T)
    d["nmlp_post"] = bc(inp["norm_mlp_post"][0])
    d["w_up"] = f(inp["w_up"][0].reshape(8, 128, D_FF).transpose(1, 0, 2))
    d["w_down"] = f(inp["w_down"][0].reshape(32, 128, D_MODEL).transpose(1, 0, 2))
    d.update(_consts(SEQ))
    return d


_NC_CACHE = {}


def kernel(**inputs):
    x = np.asarray(inputs["x"])
    B, SEQ, _ = x.shape
    if SEQ not in _NC_CACHE:
        _NC_CACHE[SEQ] = build(SEQ)
    nc = _NC_CACHE[SEQ]
    shared = prep_shared(inputs, SEQ)
    in_maps = []
    for b in range(B):
        m = dict(shared)
        m["x"] = np.ascontiguousarray(x[b], dtype=np.float32)
        in_maps.append(m)
    res = run_bass_kernel_spmd(nc, in_maps, core_ids=list(range(B)))
    out = np.stack([np.asarray(r["out"]) for r in res.results], axis=0)
    return out.astype(np.float32)
```

```python
import os
import numpy as np
from contextlib import ExitStack
import concourse.bass as bass
import concourse.mybir as mybir
from concourse.bass_utils import run_bass_kernel_spmd

F32 = mybir.dt.float32
BF16 = mybir.dt.bfloat16
AF = mybir.ActivationFunctionType
ALU = mybir.AluOpType
AX = mybir.AxisListType

D_MODEL = 1024
N_META = 16
DN_HEADS = 4
DH = 128
DN_W = 512
AT_W = 512
IN_COLS = 2832
D_FF = 4096
EPS = 1e-6
CH = 64


class _Op:
    __slots__ = ("idx", "eng", "fn", "deps", "is_dma", "dkey", "sig", "needed")

    def __init__(self, idx, eng, fn, is_dma, dkey):
        self.idx = idx
        self.eng = eng
        self.fn = fn
        self.deps = set()
        self.is_dma = is_dma
        self.dkey = dkey
        self.sig = None
        self.needed = False


class Sched:
    SEM_ROT = 30000

    def __init__(self, nc, es):
        self.nc = nc
        self.es = es
        self.ops = []
        self.res_w = {}
        self.res_r = {}
        self.engs = {"pe": nc.tensor, "act": nc.scalar, "dve": nc.vector,
                     "pool": nc.gpsimd, "sp": nc.sync}

    def add(self, eng, fn, reads=(), writes=(), dma=False, dkey=None):
        idx = len(self.ops)
        op = _Op(idx, eng, fn, dma, dkey)
        for r in reads:
            w = self.res_w.get(r)
            if w is not None:
                op.deps.add(w)
        for wr in writes:
            w = self.res_w.get(wr)
            if w is not None:
                op.deps.add(w)
            for rd in self.res_r.get(wr, ()):
                op.deps.add(rd)
        op.deps.discard(idx)
        for r in reads:
            self.res_r.setdefault(r, []).append(idx)
        for wr in writes:
            self.res_w[wr] = idx
            self.res_r[wr] = []
        self.ops.append(op)
        return idx

    def dma(self, q, out, in_, reads, writes, dkey):
        return self.add(q, lambda e: e.dma_start(out=out, in_=in_), reads, writes,
                        dma=True, dkey=dkey)

    def phase_end(self, final=False):
        ops = self.ops
        if not hasattr(self, "eng_cnt"):
            self.eng_cnt = {}
            self.dma_cnt = {}
            self.sems = {}
            self.waited = {}
            self.barrier = []
        for op in ops:
            keep = {}
            for d in op.deps:
                dop = ops[d]
                if dop.is_dma:
                    k = ("dma", dop.dkey)
                else:
                    if dop.eng == op.eng and not op.is_dma and op.eng == "pe":
                        continue
                    k = ("eng", dop.eng)
                if k not in keep or keep[k] < d:
                    keep[k] = d
            op.deps = set(keep.values())
            for d in op.deps:
                ops[d].needed = True
        last = {}
        for op in ops:
            if op.fn is None:
                continue
            k = ("dma", op.dkey) if op.is_dma else ("eng", op.eng)
            last[k] = op
        for op in last.values():
            op.needed = True

        def get_sem(key):
            if key not in self.sems:
                self.sems[key] = self.es.enter_context(
                    self.nc.semaphore("s%d" % len(self.sems)))
            return self.sems[key]

        for op in ops:
            if not op.needed:
                continue
            if op.is_dma:
                c = self.dma_cnt.get(op.dkey, 0) + 1
                self.dma_cnt[op.dkey] = c
                rot, val = divmod(c - 1, self.SEM_ROT // 16)
                op.sig = (get_sem(("dma", op.dkey, rot)), (val + 1) * 16, 16)
            else:
                c = self.eng_cnt.get(op.eng, 0) + 1
                self.eng_cnt[op.eng] = c
                rot, val = divmod(c - 1, self.SEM_ROT)
                op.sig = (get_sem(("eng", op.eng, rot)), val + 1, 1)
        waited = self.waited

        def wait(eng, sem, val):
            k = (eng, id(sem))
            if waited.get(k, 0) < val:
                self.engs[eng].wait_ge(sem, val)
                waited[k] = val

        started = set()
        for op in ops:
            e = self.engs[op.eng]
            if op.eng not in started:
                started.add(op.eng)
                for sem, val in self.barrier:
                    wait(op.eng, sem, val)
            for d in sorted(op.deps):
                sem, val, _ = ops[d].sig
                wait(op.eng, sem, val)
            if op.fn is None:
                continue
            inst = op.fn(e)
            if op.sig is not None:
                inst.then_inc(op.sig[0], op.sig[2])
        self.barrier = self.barrier + [(op.sig[0], op.sig[1]) for op in last.values()]
        mx = {}
        for sem, val in self.barrier:
            if id(sem) not in mx or mx[id(sem)][1] < val:
                mx[id(sem)] = (sem, val)
        self.barrier = list(mx.values())
        if final:
            for sem, val in self.barrier:
                wait("sp", sem, val)
        self.n_ops = getattr(self, "n_ops", 0) + len(ops)
        self.ops = []
        self.res_w = {}
        self.res_r = {}


class Ctx:
    pass


def _alloc(C, es, kind, name, shape, dt):
    if kind == "sb":
        return es.enter_context(C.nc.sbuf_tensor(name, list(shape), dt))
    return es.enter_context(C.nc.psum_tensor(name, list(shape), dt))


def build(SEQ, debug=False, phases="ABGCDE"):
    nc = bass.Bass("TRN2", target_bir_lowering=False)
    C = Ctx()
    C.nc = nc
    C.SEQ = SEQ
    C.NE2 = 64 + SEQ
    C.NB = C.NE2 // CH
    C.LK = N_META + SEQ
    C.NQB = SEQ // 128
    C.NKT = 1 + SEQ // 128
    C.debug = debug

    def din(name, shape, dt=F32):
        return nc.dram_tensor(name, list(shape), dt, kind="ExternalInput").ap()

    def dscr(name, shape, dt=F32):
        kind = "ExternalOutput" if debug else "Internal"
        return nc.dram_tensor(name, list(shape), dt, kind=kind).ap()

    D = Ctx()
    C.D = D
    D.x = din("x", [SEQ, D_MODEL])
    D.meta = din("meta", [N_META, D_MODEL])
    D.w_in = din("w_in", [128, 8, IN_COLS])
    D.nmix_pre = din("nmix_pre", [128, 8])
    D.conv_w = din("conv_w", [128, 12, 5])
    D.a_log = din("a_log", [128, 8])
    D.dt_bias = din("dt_bias", [128, 8])
    D.dn_norm = din("dn_norm", [128, 512])
    D.q_norm = din("q_norm", [128, 512])
    D.k_norm = din("k_norm", [128, 128])
    D.w_out_dn = din("w_out_dn", [128, 4, D_MODEL])
    D.w_out_at = din("w_out_at", [64, 8, D_MODEL])
    D.nmix_post = din("nmix_post", [128, D_MODEL])
    D.nmlp_pre = din("nmlp_pre", [128, 8])
    D.nmlp_post = din("nmlp_post", [128, D_MODEL])
    D.w_up = din("w_up", [128, 8, D_FF])
    D.w_down = din("w_down", [128, 32, D_MODEL])
    D.ident = din("ident", [128, 128])
    D.rope_cos = din("rope_cos", [SEQ, 256])
    D.rope_sin = din("rope_sin", [SEQ, 256])
    D.masks = din("masks", [64, 12, 64])
    D.out = nc.dram_tensor("out", [SEQ, D_MODEL], F32, kind="ExternalOutput").ap()
    D.PQKV = dscr("PQKV", [1536, C.NE2 + 4], BF16)
    D.ZS = dscr("ZS", [SEQ, 512], F32)
    D.BG = dscr("BG", [C.NE2, 16], F32)
    D.QT = dscr("QT", [2, 64, C.NQB, 4, 128], BF16)
    D.QKT = dscr("QKT", [2, 4, 128, C.NE2], BF16)
    D.KVT = dscr("KVT", [C.NE2, 2, 4, 128], BF16)
    D.OF = dscr("OF", [2, C.NE2, 512], F32)
    D.OAT = dscr("OAT", [2, 64, C.NQB, 4, 128], BF16)
    D.H1 = dscr("H1", [SEQ, D_MODEL], F32)
    if debug:
        D.KTd = dscr("KTd", [64, 2, C.LK], BF16)
        D.VAd = dscr("VAd", [128, C.NKT, 2, 128], BF16)

    with ExitStack() as es:
        S = Sched(nc, es)
        C.S = S
        C.ident_f = _alloc(C, es, "sb", "ident_f", [128, 128], F32)
        C.ident_b = _alloc(C, es, "sb", "ident_b", [128, 128], BF16)
        C.eps_t = _alloc(C, es, "sb", "eps_t", [128, 1], F32)
        C.one_t = _alloc(C, es, "sb", "one_t", [128, 1], F32)
        S.dma("sp", C.ident_f[:], D.ident[:, :], [], ["ident_f"], "ident_f")
        S.add("dve", lambda e: e.tensor_copy(C.ident_b[:], C.ident_f[:]), ["ident_f"], ["ident_b"])
        S.add("pool", lambda e: e.memset(C.eps_t[:], EPS), [], ["eps_t"])
        S.add("pool", lambda e: e.memset(C.one_t[:], 1.0), [], ["one_t"])
        with ExitStack() as kes:
            C.KT = _alloc(C, kes, "sb", "KT", [64, 2, C.LK], BF16)
            C.VA = _alloc(C, kes, "sb", "VA", [128, C.NKT, 2, 128], BF16)
            S.add("pool", lambda e: e.memset(C.VA[:], 1.0), [], ["VA"])
            S.phase_end()
            for ph, fn in (("A", phase_A), ("B", phase_B), ("G", phase_G), ("C", phase_C)):
                if ph in phases:
                    with ExitStack() as pes:
                        fn(C, pes)
                        S.phase_end()
            if debug:
                S.dma("pool", D.KTd[:, :, :], C.KT[:], ["KT"], ["KTd"], "KTd")
                S.dma("pool", D.VAd[:, :, :, :], C.VA[:], ["VA"], ["VAd"], "VAd")
                S.phase_end()
        for ph, fn in (("D", phase_D1), ("E", phase_D2)):
            if ph in phases:
                with ExitStack() as pes:
                    fn(C, pes)
                    S.phase_end()
        S.phase_end(final=True)
    return nc


def phase_A(C, es):
    nc, S, D = C.nc, C.S, C.D
    SEQ = C.SEQ
    sb = lambda n, s, d=F32: _alloc(C, es, "sb", n, s, d)
    ps = lambda n, s, d=F32: _alloc(C, es, "ps", n, s, d)
    Wb = sb("A_Wb", [128, 8, IN_COLS], BF16)
    wst = sb("A_wst", [128, IN_COLS], F32)
    nw = sb("A_nw", [128, 8], F32)
    dtb = sb("A_dtb", [128, 8], F32)
    negA = sb("A_negA", [128, 8], F32)
    qnw = sb("A_qnw", [128, 512], F32)
    knw = sb("A_knw", [128, 128], F32)
    xt = [sb("A_xt%d" % i, [128, 1024], F32) for i in range(2)]
    junk = sb("A_junk", [128, 1024], F32)
    ss = sb("A_ss", [128, 1], F32)
    rstd = sb("A_rstd", [128, 1], F32)
    xn = sb("A_xn", [128, 1024], BF16)
    xnT = sb("A_xnT", [128, 8, 512], BF16)
    qkv = sb("A_qkv", [128, 12, 512], BF16)
    zs = sb("A_zs", [128, 512], F32)
    bg = sb("A_bg", [128, 16], F32)
    t8 = sb("A_t8", [128, 8], F32)
    cos = sb("A_cos", [128, 256], F32)
    sin = sb("A_sin", [128, 256], F32)
    sq = sb("A_sq", [128, 512], F32)
    ssq = sb("A_ssq", [128, 16], F32)
    qn = sb("A_qn", [128, 512], F32)
    ra = sb("A_ra", [128, 256], F32)
    rb = sb("A_rb", [128, 256], F32)
    qr = sb("A_qr", [128, 512], BF16)
    kn = sb("A_kn", [128, 128], F32)
    kr = sb("A_kr", [128, 128], BF16)
    qt = sb("A_qt", [64, 8, 128], BF16)
    zero = sb("A_zero", [128, 12, 52], BF16)
    zero32 = sb("A_zero32", [64, 16], F32)
    pT = ps("A_pT", [128, 8, 128], BF16)
    psF = [ps("A_psF%d" % i, [128, 512], F32) for i in range(2)]
    psZ = ps("A_psZ", [128, 512], F32)
    psQ = ps("A_psQ", [128, 512], F32)
    psK = ps("A_psK", [128, 512], F32)
    psQT = ps("A_psQT", [64, 8, 128], BF16)
    psKT = ps("A_psKT", [64, 2, 128], BF16)

    S.dma("sp", nw[:], D.nmix_pre[:, :], [], ["nw"], "nw")
    S.dma("sp", dtb[:], D.dt_bias[:, :], [], ["dtb"], "dtb")
    S.dma("sp", negA[:], D.a_log[:, :], [], ["negA"], "negA")
    S.dma("sp", qnw[:], D.q_norm[:, :], [], ["qnw"], "qnw")
    S.dma("sp", knw[:], D.k_norm[:, :], [], ["knw"], "knw")
    S.add("act", lambda e: e.activation(out=negA[:], in_=negA[:], func=AF.Exp), ["negA"], ["negA"])
    S.add("dve", lambda e: e.tensor_scalar(out=negA[:], in0=negA[:], scalar1=-1.0, scalar2=None,
                                           op0=ALU.mult), ["negA"], ["negA"])
    for kc in range(8):
        S.dma("sp", wst[:], D.w_in[:, kc, :], [], ["wst"], "wst")
        S.add("dve", lambda e, kc=kc: e.tensor_scalar(out=Wb[:, kc, :], in0=wst[:], scalar1=nw[:, kc:kc + 1],
                                                      scalar2=None, op0=ALU.mult),
              ["wst", "nw"], ["Wb"])
    S.add("pool", lambda e: e.memset(zero[:], 0.0), [], ["zero"])
    S.add("pool", lambda e: e.memset(zero32[:], 0.0), [], ["zero32"])
    pq = D.PQKV.rearrange("(c p) n -> p c n", p=128)
    S.dma("pool", pq[:, :, 0:50], zero[:, :, 0:50], ["zero"], ["PQKV"], "zero")
    S.dma("pool", pq[:, :, C.NE2 + 2:C.NE2 + 4], zero[:, :, 50:52], ["zero"], ["PQKV"], "zero")
    S.dma("pool", D.BG[0:48, :], zero32[0:48, :], ["zero32"], ["BG"], "zero32")

    def norm_head(psrc, nh, dst_n, w_bc, rkey):
        P = rkey
        W = nh * 64
        S.add("act", lambda e: e.activation(out=sq[0:P, 0:W], in_=psrc, func=AF.Square),
              ["psK" if nh == 2 else "psQ"], ["sq"])
        S.add("dve", lambda e: e.tensor_reduce(out=ssq[0:P, 0:nh],
                                               in_=sq[0:P, 0:W].rearrange("p (h d) -> p h d", d=64),
                                               op=ALU.add, axis=AX.X), ["sq"], ["ssq"])
        S.add("act", lambda e: e.activation(out=ssq[0:P, 0:nh], in_=ssq[0:P, 0:nh], func=AF.Sqrt,
                                            scale=1.0 / 64, bias=C.eps_t[0:P, :]), ["ssq"], ["ssq"])
        S.add("dve", lambda e: e.reciprocal(out=ssq[0:P, 0:nh], in_=ssq[0:P, 0:nh]), ["ssq"], ["ssq"])
        S.add("dve", lambda e: e.tensor_tensor(
            out=dst_n[0:P, 0:W].rearrange("p (h d) -> p h d", d=64),
            in0=psrc.rearrange("p (h d) -> p h d", d=64),
            in1=ssq[0:P, 0:nh].unsqueeze(2).broadcast_to([P, nh, 64]), op=ALU.mult),
            ["ssq", "psK" if nh == 2 else "psQ"], ["dstn%d" % nh])
        S.add("dve", lambda e: e.tensor_tensor(out=dst_n[0:P, 0:W], in0=dst_n[0:P, 0:W],
                                               in1=w_bc[0:P, 0:W], op=ALU.mult),
              ["dstn%d" % nh, "qnw", "knw"], ["dstn%d" % nh])

    def rope(src, dst, nh, P, do_rope):
        W = nh * 64
        key = "dstn%d" % nh
        okey = "rot%d" % nh
        if not do_rope:
            S.add("dve", lambda e: e.tensor_copy(dst[0:P, 0:W], src[0:P, 0:W]), [key], [okey])
            return
        H2 = nh * 2
        sv = src[0:P, 0:W].rearrange("p (a t f) -> p a t f", t=2, f=16)
        dv = dst[0:P, 0:W].rearrange("p (a t f) -> p a t f", t=2, f=16)
        x1, x2 = sv[:, :, 0, :], sv[:, :, 1, :]
        o1, o2 = dv[:, :, 0, :], dv[:, :, 1, :]
        cv = cos[0:P, 0:H2 * 16].rearrange("p (a f) -> p a f", f=16)
        sn = sin[0:P, 0:H2 * 16].rearrange("p (a f) -> p a f", f=16)
        av = ra[0:P, 0:H2 * 16].rearrange("p (a f) -> p a f", f=16)
        bv = rb[0:P, 0:H2 * 16].rearrange("p (a f) -> p a f", f=16)
        tt = lambda o, a, b, op: (lambda e: e.tensor_tensor(out=o, in0=a, in1=b, op=op))
        S.add("dve", tt(av, x1, cv, ALU.mult), [key, "cos"], ["ra"])
        S.add("dve", tt(bv, x2, sn, ALU.mult), [key, "sin"], ["rb"])
        S.add("dve", tt(o1, av, bv, ALU.subtract), ["ra", "rb"], [okey])
        S.add("dve", tt(av, x2, cv, ALU.mult), [key, "cos", okey], ["ra"])
        S.add("dve", tt(bv, x1, sn, ALU.mult), [key, "sin", okey], ["rb"])
        S.add("dve", tt(o2, av, bv, ALU.add), ["ra", "rb"], [okey])

    def do_tile(ti, xsrc, P, c0, r0, k0, kt):
        b = ti % 2
        X = xt[b]
        xk = "xt%d" % b
        S.dma("sp", X[0:P, :], xsrc, [], [xk], xk)
        S.add("act", lambda e: e.activation(out=junk[0:P, :], in_=X[0:P, :], func=AF.Square,
                                            accum_out=ss[0:P, :]), [xk], ["junk", "ss"])
        S.add("act", lambda e: e.activation(out=ss[0:P, :], in_=ss[0:P, :], func=AF.Sqrt,
                                            scale=1.0 / D_MODEL, bias=C.eps_t[0:P, :]), ["ss"], ["ss"])
        S.add("dve", lambda e: e.reciprocal(out=rstd[0:P, :], in_=ss[0:P, :]), ["ss"], ["rstd"])
        S.add("act", lambda e: e.activation(out=xn[0:P, :], in_=X[0:P, :], func=AF.Copy,
                                            scale=rstd[0:P, 0:1]), [xk, "rstd"], ["xn"])
        for kc in range(8):
            S.add("pe", lambda e, kc=kc: e.transpose(pT[:, kc, 0:P], xn[0:P, kc * 128:(kc + 1) * 128],
                                                     C.ident_b[0:P, 0:P]), ["xn", "ident_b"], ["pT"])
        S.add("dve", lambda e: e.tensor_copy(xnT[:, :, c0:c0 + P], pT[:, :, 0:P]), ["pT"], ["xnT"])
        def tm(dst, col0, ncol, key, first=True):
            for kc in range(8):
                S.add("pe", lambda e, kc=kc: e.matmul(dst, xnT[:, kc, c0:c0 + P], Wb[:, kc, col0:col0 + ncol],
                                                      start=(kc == 0), stop=(kc == 7)),
                      ["xnT", "Wb"], [key])
        if r0 is not None:
            tm(psZ[0:P, :], 1536, 512, "psZ")
            S.add("act", lambda e: e.activation(out=zs[0:P, :], in_=psZ[0:P, :], func=AF.Silu), ["psZ"], ["zs"])
            S.dma("pool", D.ZS[r0:r0 + P, :], zs[0:P, :], ["zs"], ["ZS"], "zs")
            tm(psQ[0:P, :], 2064, 512, "psQ")
        tm(psK[0:P, 0:256], 2576, 256, "psK")
        tm(psK[0:P, 256:272], 2048, 16, "psK")
        S.add("act", lambda e: e.activation(out=bg[0:P, 0:8], in_=psK[0:P, 256:264], func=AF.Sigmoid),
              ["psK"], ["bg"])
        S.add("dve", lambda e: e.tensor_tensor(out=t8[0:P, :], in0=psK[0:P, 264:272], in1=dtb[0:P, :],
                                               op=ALU.add), ["psK", "dtb"], ["t8"])
        S.add("act", lambda e: e.activation(out=t8[0:P, :], in_=t8[0:P, :], func=AF.Exp), ["t8"], ["t8"])
        S.add("act", lambda e: e.activation(out=t8[0:P, :], in_=t8[0:P, :], func=AF.Ln, bias=C.one_t[0:P, :]),
              ["t8"], ["t8"])
        S.add("dve", lambda e: e.tensor_tensor(out=bg[0:P, 8:16], in0=t8[0:P, :], in1=negA[0:P, :],
                                               op=ALU.mult), ["t8", "negA", "bg"], ["bg"])
        e2 = 48 + k0
        S.dma("pool", D.BG[e2:e2 + P, :], bg[0:P, :], ["bg"], ["BG"], "bg")
        if r0 is not None:
            S.dma("sp", cos[0:P, :], D.rope_cos[r0:r0 + P, :], [], ["cos"], "cos")
            S.dma("sp", sin[0:P, :], D.rope_sin[r0:r0 + P, :], [], ["sin"], "sin")
            norm_head(psQ[0:P, :], 8, qn, qnw, P)
            rope(qn, qr, 8, P, True)
            for h in range(8):
                S.add("pe", lambda e, h=h: e.transpose(psQT[:, h, 0:P], qr[0:P, h * 64:(h + 1) * 64],
                                                       C.ident_b[0:P, 0:P]), ["rot8", "ident_b"], ["psQT"])
            S.add("act", lambda e: e.activation(out=qt[:, :, 0:P], in_=psQT[:, :, 0:P], func=AF.Copy),
                  ["psQT"], ["qt"])
            qb = r0 // 128
            S.dma("pool", D.QT[:, :, qb, :, :].rearrange("k d g q -> d k g q"),
                  qt[:].rearrange("d (k g) q -> d k g q", k=2), ["qt"], ["QT"], "qt")
        norm_head(psK[0:P, 0:128], 2, kn, knw, P)
        rope(kn, kr, 2, P, r0 is not None)
        for h in range(2):
            S.add("pe", lambda e, h=h: e.transpose(psKT[:, h, 0:P], kr[0:P, h * 64:(h + 1) * 64],
                                                   C.ident_b[0:P, 0:P]), ["rot2", "ident_b"], ["psKT"])
        S.add("act", lambda e: e.activation(out=C.KT[:, :, k0:k0 + P], in_=psKT[:, :, 0:P], func=AF.Copy),
              ["psKT"], ["KT"])
        S.add("act", lambda e: e.activation(
            out=C.VA[0:P, kt, :, 0:64], in_=psK[0:P, 128:256].rearrange("p (h d) -> p h d", d=64),
            func=AF.Copy), ["psK"], ["VA"])

    def feat_major(T, e2_0):
        for fc in range(12):
            pf = psF[fc % 2]
            pk = "psF%d" % (fc % 2)
            for kc in range(8):
                S.add("pe", lambda e, kc=kc, fc=fc, pf=pf: e.matmul(
                    pf[:, 0:T], Wb[:, kc, fc * 128:(fc + 1) * 128], xnT[:, kc, 0:T],
                    start=(kc == 0), stop=(kc == 7)), ["xnT", "Wb"], [pk])
            if fc % 2 == 0:
                S.add("act", lambda e, fc=fc, pf=pf: e.activation(out=qkv[:, fc, 0:T], in_=pf[:, 0:T],
                                                                  func=AF.Copy), [pk], ["qkv"])
            else:
                S.add("dve", lambda e, fc=fc, pf=pf: e.tensor_copy(qkv[:, fc, 0:T], pf[:, 0:T]),
                      [pk], ["qkv"])
        S.dma("pool", pq[:, :, 2 + e2_0:2 + e2_0 + T], qkv[:, :, 0:T], ["qkv"], ["PQKV"], "qkv")

    do_tile(0, D.meta[:, :], N_META, 0, None, 0, 0)
    feat_major(N_META, 48)
    ti = 1
    for st in range(SEQ // 512):
        for i in range(4):
            r0 = st * 512 + i * 128
            do_tile(ti, D.x[r0:r0 + 128, :], 128, i * 128, r0, N_META + r0, 1 + r0 // 128)
            ti += 1
        feat_major(512, 64 + st * 512)


def phase_B(C, es):
    nc, S, D = C.nc, C.S, C.D
    NE2 = C.NE2
    sb = lambda n, s, d=F32: _alloc(C, es, "sb", n, s, d)
    ps = lambda n, s, d=F32: _alloc(C, es, "ps", n, s, d)
    cw = sb("B_cw", [128, 12, 5], F32)
    Dg = sb("B_Dg", [128, 12, 5, 128], BF16)
    ones = sb("B_ones", [128, 128], BF16)
    pin = [sb("B_pin%d" % i, [128, 12, 516], BF16) for i in range(2)]
    ysb = sb("B_ysb", [128, 8, 512], F32)
    ssb = sb("B_ssb", [128, 8, 512], F32)
    sqb = [sb("B_sqb%d" % i, [128, 512], BF16) for i in range(2)]
    qk_o = sb("B_qko", [128, 8, 512], BF16)
    kv_f = sb("B_kvf", [128, 8, 512], BF16)
    kv_t = [sb("B_kvt%d" % i, [128, 8, 128], BF16) for i in range(2)]
    psC = [ps("B_psC%d" % i, [128, 512], F32) for i in range(2)]
    psN = [ps("B_psN%d" % i, [128, 512], F32) for i in range(2)]
    psT = ps("B_psT", [128, 8, 128], BF16)
    S.dma("sp", cw[:], D.conv_w[:, :, :], [], ["cw"], "cw")
    S.add("pool", lambda e: e.memset(ones[:], 1.0), [], ["ones"])
    for fc in range(12):
        for j in range(5):
            S.add("dve", lambda e, fc=fc, j=j: e.tensor_scalar(
                out=Dg[:, fc, j, :], in0=C.ident_f[:], scalar1=cw[:, fc, j:j + 1], scalar2=None,
                op0=ALU.mult), ["cw", "ident_f"], ["Dg"])
    pq = D.PQKV.rearrange("(c p) n -> p c n", p=128)
    qkt_d = D.QKT.rearrange("a h d n -> d (a h) n")
    groups = [(g0, 512) for g0 in range(0, NE2 - 64, 512)] + [(NE2 - 64, 64)]
    def do_group(gi, g0, T):
        P_ = pin[gi % 2]
        pk = "pin%d" % (gi % 2)
        S.dma("sp", P_[:, :, 0:T + 4], pq[:, :, g0:g0 + T + 4], ["PQKV"], [pk], pk)
        for fc in range(12):
            pc = psC[fc % 2]
            pck = "psC%d" % (fc % 2)
            for j in range(5):
                S.add("pe", lambda e, fc=fc, j=j, pc=pc, P_=P_: e.matmul(
                    pc[:, 0:T], Dg[:, fc, j, :], P_[:, fc, j:j + T], start=(j == 0), stop=(j == 4)),
                    [pk, "Dg"], [pck])
            if fc < 8:
                S.add("act", lambda e, fc=fc, pc=pc: e.activation(out=ysb[:, fc, 0:T], in_=pc[:, 0:T],
                                                                  func=AF.Silu), [pck], ["ysb"])
                sq_ = sqb[fc % 2]
                sqk = "sqb%d" % (fc % 2)
                S.add("pool", lambda e, fc=fc, sq_=sq_: e.tensor_tensor(
                    out=sq_[:, 0:T], in0=ysb[:, fc, 0:T], in1=ysb[:, fc, 0:T], op=ALU.mult),
                    ["ysb"], [sqk])
                pn = psN[fc % 2]
                pnk = "psN%d" % (fc % 2)
                S.add("pe", lambda e, pn=pn, sq_=sq_: e.matmul(pn[:, 0:T], ones[:], sq_[:, 0:T],
                                                               start=True, stop=True), [sqk, "ones"], [pnk])
                S.add("dve", lambda e, fc=fc, pn=pn: e.tensor_scalar(
                    out=ssb[:, fc, 0:T], in0=pn[:, 0:T], scalar1=EPS, scalar2=None, op0=ALU.add),
                    [pnk], ["ssb"])
            else:
                S.add("act", lambda e, fc=fc, pc=pc: e.activation(out=kv_f[:, fc - 4, 0:T], in_=pc[:, 0:T],
                                                                  func=AF.Silu), [pck], ["kvf"])
        S.add("act", lambda e: e.activation(out=ssb[:, :, 0:T], in_=ssb[:, :, 0:T], func=AF.Ln),
              ["ssb"], ["ssb"])
        S.add("act", lambda e: e.activation(out=ssb[:, :, 0:T], in_=ssb[:, :, 0:T], func=AF.Exp, scale=-0.5),
              ["ssb"], ["ssb"])
        S.add("dve", lambda e: e.scalar_tensor_tensor(
            out=qk_o[:, 0:4, 0:T], in0=ysb[:, 0:4, 0:T], scalar=float(DH ** -0.5), in1=ssb[:, 0:4, 0:T],
            op0=ALU.mult, op1=ALU.mult), ["ysb", "ssb"], ["qko"])
        S.add("dve", lambda e: e.tensor_tensor(out=qk_o[:, 4:8, 0:T], in0=ysb[:, 4:8, 0:T],
                                               in1=ssb[:, 4:8, 0:T], op=ALU.mult), ["ysb", "ssb"], ["qko"])
        S.add("pool", lambda e: e.tensor_copy(kv_f[:, 0:4, 0:T], qk_o[:, 4:8, 0:T]), ["qko"], ["kvf"])
        S.dma("pool", qkt_d[:, :, g0:g0 + T], qk_o[:, :, 0:T], ["qko"], ["QKT"], "qko")
        for sbk in range(max(T // 128, 1)):
            do_sub(g0, T, sbk)

    def do_sub(g0, T, sbk):
        if True:
            P = min(T, 128)
            c0 = sbk * 128
            kt_ = kv_t[sbk % 2]
            ktk = "kvt%d" % (sbk % 2)
            for a in range(8):
                S.add("pe", lambda e, a=a, c0=c0, P=P: e.transpose(psT[0:P, a, :], kv_f[:, a, c0:c0 + P],
                                                                 C.ident_b[:, :]), ["kvf", "ident_b"], ["psT"])
            S.add("act", lambda e, kt_=kt_, P=P: e.activation(out=kt_[0:P, :, :], in_=psT[0:P, :, :],
                                                              func=AF.Copy), ["psT"], [ktk])
            S.dma("pool", D.KVT[g0 + c0:g0 + c0 + P, :, :, :].rearrange("n a h d -> n (a h) d"),
                  kt_[0:P, :, :], [ktk], ["KVT"], ktk)

    for gi, (g0, T) in enumerate(groups):
        do_group(gi, g0, T)


def phase_G(C, es):
    nc, S, D = C.nc, C.S, C.D
    NB = C.NB
    sb = lambda n, s, d=F32: _alloc(C, es, "sb", n, s, d)
    ps = lambda n, s, d=F32: _alloc(C, es, "ps", n, s, d)
    masks = sb("G_masks", [64, 12, 64], F32)
    ones64 = sb("G_ones64", [64, 128], F32)
    S.dma("sp", masks[:], D.masks[:, :, :], [], ["masks"], "masks")
    S.add("pool", lambda e: e.memset(ones64[:], 1.0), [], ["ones64"])
    I4 = masks[:, 8:12, :]
    idf = C.ident_f[0:64, 0:64]
    B = []
    for d in range(2):
        b = Ctx()
        n = lambda x: "G%d_%s" % (d, x)
        b.qk_in = [sb(n("qkin%d" % i), [128, 4, 2, 64], BF16) for i in range(2)]
        b.kv_in = [sb(n("kvin%d" % i), [64, 2, 4, 128], BF16) for i in range(2)]
        b.bgc = [sb(n("bgc%d" % i), [64, 16], F32) for i in range(2)]
        b.Gle = sb(n("Gle"), [64, 4, 64])
        b.egr = sb(n("egr"), [128, 4, 64])
        b.decT = sb(n("decT"), [64, 4, 64])
        b.g3s = sb(n("g3s"), [128, 8])
        b.sc = sb(n("sc"), [64, 16])
        b.glast = sb(n("glast"), [128, 4])
        b.PR = [sb(n("PR%d" % i), [64, 4, 2, 64]) for i in range(2)]
        b.PTb = [sb(n("PTb%d" % i), [64, 4, 64]) for i in range(2)]
        b.rhs = sb(n("rhs"), [64, 4, 256])
        b.wtok = sb(n("wtok"), [64, 4, 128])
        b.decTi = sb(n("decTi"), [64, 4, 64])
        b.qkTm = sb(n("qkTm"), [64, 4, 64], BF16)
        b.qdT = sb(n("qdT"), [128, 4, 64], BF16)
        b.kdec = sb(n("kdec"), [64, 4, 128], BF16)
        b.wT = sb(n("wT"), [128, 4, 64], BF16)
        b.u = sb(n("u"), [64, 4, 128])
        b.vnew = sb(n("vnew"), [64, 4, 128], BF16)
        b.osb = sb(n("osb"), [64, 4, 128])
        b.Sf = sb(n("Sf"), [128, 4, 128])
        b.Sb = sb(n("Sb"), [128, 4, 128], BF16)
        b.pb = [ps(n("pb%d" % i), [128, 512], F32) for i in range(4)]
        S.add("pool", lambda e, b=b: e.memset(b.Sf[:], 0.0), [], [n("Sf")])
        S.add("pool", lambda e, b=b: e.memset(b.Sb[:], 0.0), [], [n("Sb")])
        B.append(b)
    qkt_k = D.QKT[1, :, :, :].rearrange("h d n -> d h n")
    qkt_q = D.QKT[0, :, :, :].rearrange("h d n -> d h n")

    def unit(d, s_, j, typ):
        b = B[d]
        r = lambda x: "G%d_%s" % (d, x)
        M1, M2, NS = (0, 1, 6) if typ == "F" else (2, 3, 7)
        bcol = 0 if d == 0 else 4
        gcol = 8 if d == 0 else 12
        n0 = 64 * j
        par = s_ % 2
        qk_in, kv_in, bgc = b.qk_in[par], b.kv_in[par], b.bgc[par]
        rq, rk, rb = r("qkin%d" % par), r("kvin%d" % par), r("bgc%d" % par)
        b0, b1, b2, b3 = b.pb
        p0, p1, p2, p3 = r("pb0"), r("pb1"), r("pb2"), r("pb3")
        fl = lambda t: t[:].rearrange("p h c -> p (h c)")
        S.dma("sp", qk_in[:, :, 0, :], qkt_k[:, :, n0:n0 + 64], ["QKT"], [rq], rq)
        S.dma("sp", qk_in[:, :, 1, :], qkt_q[:, :, n0:n0 + 64], ["QKT"], [rq], rq)
        S.dma("sp", kv_in[:], D.KVT[n0:n0 + 64, :, :, :], ["KVT"], [rk], rk)
        S.dma("sp", bgc[:], D.BG[n0:n0 + 64, :], ["BG"], [rb], rb)
        yield
        g4 = bgc[:, gcol:gcol + 4]
        be4 = bgc[:, bcol:bcol + 4]
        S.add("dve", lambda e: e.tensor_tensor(
            out=b.Gle[:], in0=masks[:, M1, :].unsqueeze(1).broadcast_to([64, 4, 64]),
            in1=g4.unsqueeze(2).broadcast_to([64, 4, 64]), op=ALU.mult), [rb, "masks"], [r("Gle")])
        S.add("dve", lambda e: e.tensor_scalar(out=b.sc[:, 8:12], in0=be4, scalar1=-1.0, scalar2=None,
                                               op0=ALU.mult), [rb], [r("sc_nb")])
        S.add("pool", lambda e: e.tensor_copy(b.PR[0][:, :, 1, :], I4), ["masks"], [r("PR0")])
        S.add("pool", lambda e: e.tensor_copy(b.rhs[:, :, 0:128], kv_in[:, 1, :, :]), [rk], [r("rhs")])
        yield
        Gf = fl(b.Gle)
        I4f = I4.rearrange("p h c -> p (h c)")
        S.add("pe", lambda e: e.matmul(b0[:, 0:256], ones64[:, :], Gf, start=True, stop=True),
              [r("Gle"), "ones64"], [p0])
        S.add("pe", lambda e: e.matmul(b0[0:64, 256:512], masks[:, NS, :], I4f, start=True, stop=False),
              ["masks"], [p0])
        S.add("pe", lambda e: e.matmul(b0[0:64, 256:512], masks[:, M2, :], Gf, start=False, stop=True),
              [r("Gle"), "masks"], [p0])
        S.add("pe", lambda e: e.matmul(b1[:, 256:260], ones64[:, :], g4, start=True, stop=True),
              [rb, "ones64"], [p1])
        S.add("pe", lambda e: e.matmul(b1[0:64, 260:264], masks[:, M1, :], g4, start=True, stop=True),
              [rb, "masks"], [p1])
        for h in range(4):
            S.add("pe", lambda e, h=h: e.matmul(b2[0:64, h * 128:(h + 1) * 128], qk_in[:, h, 0, :],
                                                qk_in[:, h, :, :].rearrange("p a c -> p (a c)"),
                                                start=True, stop=True), [rq], [p2])
        yield
        S.add("act", lambda e: e.activation(out=fl(b.egr), in_=b0[:, 0:256], func=AF.Exp), [p0], [r("egr")])
        S.add("act", lambda e: e.activation(out=fl(b.decT), in_=b0[0:64, 256:512], func=AF.Exp), [p0], [r("decT")])
        S.add("dve", lambda e: e.tensor_copy(b.g3s[:, 0:4], b1[:, 256:260]), [p1], [r("g3s")])
        S.add("dve", lambda e: e.tensor_copy(b.g3s[0:64, 4:8], b1[0:64, 260:264]), [p1], [r("g3s")])
        S.add("act", lambda e: e.activation(out=b.sc[:, 0:4], in_=b.g3s[0:64, 4:8], func=AF.Exp),
              [r("g3s")], [r("sc_eg")])
        S.add("act", lambda e: e.activation(out=b.glast[:], in_=b.g3s[:, 0:4], func=AF.Exp),
              [r("g3s")], [r("glast")])
        S.add("dve", lambda e: e.tensor_tensor(out=b.sc[:, 4:8], in0=b.g3s[0:64, 0:4], in1=b.g3s[0:64, 4:8],
                                               op=ALU.subtract), [r("g3s")], [r("sc_kd")])
        S.add("act", lambda e: e.activation(out=b.sc[:, 4:8], in_=b.sc[:, 4:8], func=AF.Exp),
              [r("sc_kd")], [r("sc_kd")])
        yield
        bc = lambda ap: ap.unsqueeze(2).broadcast_to([64, 4, 128])
        kq = b2[0:64, :].rearrange("p (h a c) -> p h a c", h=4, a=2)
        for h in range(4):
            S.add("dve", lambda e, h=h: e.scalar_tensor_tensor(
                out=b.PR[0][:, h, 0, :], in0=kq[:, h, 0, :], scalar=b.sc[:, 8 + h:9 + h], in1=b.decT[:, h, :],
                op0=ALU.mult, op1=ALU.mult), [p2, r("sc_nb"), r("decT")], [r("PR0")])
        S.add("pool", lambda e: e.tensor_tensor(out=b.rhs[:, :, 128:256], in0=kv_in[:, 0, :, :],
                                                in1=bc(b.sc[:, 0:4]), op=ALU.mult), [rk, r("sc_eg")], [r("rhs")])
        S.add("pool", lambda e: e.tensor_tensor(out=b.kdec[:], in0=kv_in[:, 0, :, :], in1=bc(b.sc[:, 4:8]),
                                                op=ALU.mult), [rk, r("sc_kd")], [r("kdec")])
        S.add("pool", lambda e: e.tensor_tensor(out=fl(b.decTi), in0=fl(b.decT), in1=I4f, op=ALU.add),
              [r("decT"), "masks"], [r("decTi")])
        S.add("pool", lambda e: e.tensor_tensor(out=b.qdT[:], in0=qk_in[:, :, 1, :], in1=b.egr[:], op=ALU.mult),
              [rq, r("egr")], [r("qdT")])
        yield
        for h in range(4):
            S.add("pe", lambda e, h=h: e.transpose(b3[0:64, h * 64:(h + 1) * 64], b.PR[0][:, h, 0, :], idf),
                  [r("PR0"), "ident_f"], [p3])
        yield
        S.add("dve", lambda e: e.tensor_copy(fl(b.PTb[0]), b3[0:64, 0:256]), [p3], [r("PT0")])
        S.add("dve", lambda e: e.tensor_tensor(out=b.qkTm[:], in0=kq[:, :, 1, :], in1=b.decTi[:], op=ALU.mult),
              [p2, r("decTi")], [r("qkTm")])
        yield
        for k in range(6):
            cur, nxt = k % 2, (k + 1) % 2
            bX, pX = (b0, p0) if k % 2 == 0 else (b1, p1)
            bY, pY = (b2, p2) if k % 2 == 0 else (b3, p3)
            PRc, PRn = b.PR[cur], b.PR[nxt]
            PTc, PTn = b.PTb[cur], b.PTb[nxt]
            rPRc, rPRn, rPTc, rPTn = r("PR%d" % cur), r("PR%d" % nxt), r("PT%d" % cur), r("PT%d" % nxt)
            for h in range(4):
                if k < 5:
                    S.add("pe", lambda e, h=h, bX=bX, PRc=PRc, PTc=PTc: e.matmul(
                        bX[0:64, h * 128:(h + 1) * 128], PTc[:, h, :],
                        PRc[:, h, :, :].rearrange("p a c -> p (a c)"), start=True, stop=True),
                        [rPRc, rPTc], [pX])
                    S.add("pe", lambda e, h=h, bY=bY, PRc=PRc, PTc=PTc: e.matmul(
                        bY[0:64, h * 64:(h + 1) * 64], PRc[:, h, 0, :], PTc[:, h, :], start=True, stop=True),
                        [rPRc, rPTc], [pY])
                else:
                    S.add("pe", lambda e, h=h, bX=bX, PRc=PRc, PTc=PTc: e.matmul(
                        bX[0:64, h * 128 + 64:(h + 1) * 128], PTc[:, h, :], PRc[:, h, 1, :],
                        start=True, stop=True), [rPRc, rPTc], [pX])
            yield
            xv = bX[0:64, :].rearrange("p (h a c) -> p h a c", h=4, a=2)
            if k < 5:
                S.add("dve", lambda e, xv=xv, PRn=PRn: e.tensor_copy(PRn[:, :, 0, :], xv[:, :, 0, :]),
                      [pX], [rPRn])
                S.add("dve", lambda e, bY=bY, PTn=PTn: e.tensor_copy(fl(PTn), bY[0:64, 0:256]), [pY], [rPTn])
            S.add("dve", lambda e, xv=xv, PRn=PRn, PRc=PRc: e.tensor_tensor(
                out=PRn[:, :, 1, :], in0=xv[:, :, 1, :], in1=PRc[:, :, 1, :], op=ALU.add), [pX, rPRc], [rPRn])
            yield
        X = b.PR[0]
        rX = r("PR0")
        for h in range(4):
            bk, pk_ = (b1, p1) if h < 2 else (b3, p3)
            S.add("pe", lambda e, h=h, bk=bk: e.matmul(bk[0:64, (h % 2) * 256:(h % 2 + 1) * 256], X[:, h, 1, :],
                                                      b.rhs[:, h, :], start=True, stop=True), [r("rhs"), rX], [pk_])
        yield
        for hp in range(2):
            bk, pk_ = (b1, p1) if hp == 0 else (b3, p3)
            v4 = bk[0:64, :].rearrange("p (h a c) -> p h a c", h=2, a=2)
            bb = be4[:, hp * 2:hp * 2 + 2].unsqueeze(2).broadcast_to([64, 2, 128])
            S.add("dve", lambda e, v4=v4, bb=bb, hp=hp: e.tensor_tensor(
                out=b.u[:, hp * 2:hp * 2 + 2, :], in0=v4[:, :, 0, :], in1=bb, op=ALU.mult), [pk_, rb], [r("u")])
            S.add("dve", lambda e, v4=v4, bb=bb, hp=hp: e.tensor_tensor(
                out=b.wtok[:, hp * 2:hp * 2 + 2, :], in0=v4[:, :, 1, :], in1=bb, op=ALU.mult), [pk_, rb], [r("wtok")])
        yield
        for h in range(4):
            S.add("pe", lambda e, h=h: e.transpose(b2[:, h * 64:(h + 1) * 64], b.wtok[:, h, :], idf),
                  [r("wtok"), "ident_f"], [p2])
        yield
        S.add("dve", lambda e: e.tensor_copy(fl(b.wT), b2[:, 0:256]), [p2], [r("wT")])
        yield
        for h in range(4):
            S.add("pe", lambda e, h=h: e.matmul(b0[0:64, h * 128:(h + 1) * 128], b.wT[:, h, :], b.Sb[:, h, :],
                                                start=True, stop=True), [r("wT"), r("Sb")], [p0])
        yield
        S.add("dve", lambda e: e.tensor_tensor(out=fl(b.vnew), in0=fl(b.u), in1=b0[0:64, :],
                                               op=ALU.subtract), [r("u"), p0], [r("vnew")])
        yield
        for h in range(4):
            S.add("pe", lambda e, h=h: e.matmul(b2[0:64, h * 128:(h + 1) * 128], b.qdT[:, h, :], b.Sb[:, h, :],
                                                start=True, stop=False), [r("qdT"), r("Sb")], [p2])
            S.add("pe", lambda e, h=h: e.matmul(b2[0:64, h * 128:(h + 1) * 128], b.qkTm[:, h, :], b.vnew[:, h, :],
                                                start=False, stop=True), [r("qkTm"), r("vnew")], [p2])
        for h in range(4):
            S.add("pe", lambda e, h=h: e.matmul(b1[:, h * 128:(h + 1) * 128], b.kdec[:, h, :], b.vnew[:, h, :],
                                                start=True, stop=True), [r("kdec"), r("vnew")], [p1])
        yield
        S.add("dve", lambda e: e.tensor_copy(fl(b.osb), b2[0:64, :]), [p2], [r("osb")])
        S.dma("pool", D.OF[d, n0:n0 + 64, :], fl(b.osb), [r("osb")], ["OF"], r("osb"))
        for h in range(4):
            S.add("dve", lambda e, h=h: e.scalar_tensor_tensor(
                out=b.Sf[:, h, :], in0=b.Sf[:, h, :], scalar=b.glast[:, h:h + 1], in1=b1[:, h * 128:(h + 1) * 128],
                op0=ALU.mult, op1=ALU.add), [r("Sf"), r("glast"), p1], [r("Sf")])
        S.add("act", lambda e: e.activation(out=fl(b.Sb), in_=fl(b.Sf), func=AF.Copy), [r("Sf")], [r("Sb")])
        yield

    gstop = int(os.environ.get("GSTOP", "1000"))
    gnb = int(os.environ.get("GNB", str(NB)))
    for s_ in range(min(NB, gnb)):
        jb = 0 if s_ == 0 else NB - s_
        gens = [unit(0, s_, s_, "F"), unit(1, s_, jb, "F" if s_ == 0 else "R")]
        alive = [True, True]
        cnt = 0
        while any(alive) and cnt < gstop:
            cnt += 1
            for gi, g in enumerate(gens):
                if alive[gi]:
                    try:
                        next(g)
                    except StopIteration:
                        alive[gi] = False


def phase_C(C, es):
    nc, S, D = C.nc, C.S, C.D
    sb = lambda n, s, d=F32: _alloc(C, es, "sb", n, s, d)
    ps = lambda n, s, d=F32: _alloc(C, es, "ps", n, s, d)
    NKT = C.NKT
    qtb = [sb("C_qtb%d" % i, [64, 512], BF16) for i in range(2)]
    pT = [sb("C_pT%d" % i, [128, 512], BF16) for i in range(3)]
    rinv = sb("C_rinv", [128, 512], F32)
    oT = [sb("C_oT%d" % i, [64, 512], BF16) for i in range(2)]
    psS = [ps("C_psS%d" % i, [128, 512], F32) for i in range(3)]
    psO = [ps("C_psO%d" % i, [128, 512], F32) for i in range(2)]

    blocks = [(qb, kvh) for qb in range(C.NQB) for kvh in range(2)]
    iters = [(bi, kt) for bi in range(len(blocks)) for kt in range(NKT)]

    def load_q(bi):
        qb, kvh = blocks[bi]
        S.dma("sp", qtb[bi % 2][:], D.QT[kvh, :, qb, :, :].rearrange("d g q -> d (g q)"), ["QT"],
              ["qtb%d" % (bi % 2)], "qtb%d" % (bi % 2))

    def geom(kt):
        nk = N_META if kt == 0 else 128
        kc0 = 0 if kt == 0 else N_META + (kt - 1) * 128
        return nk, kc0

    def s_mm(n):
        bi, kt = iters[n]
        qb, kvh = blocks[bi]
        nk, kc0 = geom(kt)
        pS, pSk = psS[n % 3], "psS%d" % (n % 3)
        q_, qk = qtb[bi % 2], "qtb%d" % (bi % 2)
        S.add("pe", lambda e: e.matmul(pS[0:nk, :], C.KT[:, kvh, kc0:kc0 + nk], q_[:], start=True, stop=True),
              ["KT", qk], [pSk])

    def exp_pv(n):
        bi, kt = iters[n]
        qb, kvh = blocks[bi]
        nk, kc0 = geom(kt)
        pS, pSk = psS[n % 3], "psS%d" % (n % 3)
        p_, pk = pT[n % 3], "pT%d" % (n % 3)
        po, pok = psO[bi % 2], "psO%d" % (bi % 2)
        S.add("act", lambda e: e.activation(out=p_[0:nk, :], in_=pS[0:nk, :], func=AF.Exp, scale=0.125),
              [pSk], [pk])
        S.add("pe", lambda e: e.matmul(po[:, :], C.VA[0:nk, kt, kvh, :], p_[0:nk, :], start=(kt == 0),
                                       stop=(kt == NKT - 1)), ["VA", pk], [pok])

    def finish(bi):
        qb, kvh = blocks[bi]
        po, pok = psO[bi % 2], "psO%d" % (bi % 2)
        o_, ok = oT[bi % 2], "oT%d" % (bi % 2)
        S.add("dve", lambda e: e.reciprocal(out=rinv[64:128, :], in_=po[64:128, :]), [pok], ["rinv"])
        S.add("dve", lambda e: e.tensor_tensor(out=o_[:], in0=po[0:64, :], in1=rinv[64:128, :], op=ALU.mult),
              [pok, "rinv"], [ok])
        S.dma("pool", D.OAT[kvh, :, qb, :, :].rearrange("d g q -> d (g q)"), o_[:], [ok], ["OAT"], ok)

    NI = len(iters)
    load_q(0)
    if len(blocks) > 1:
        load_q(1)
    s_mm(0)
    if NI > 1:
        s_mm(1)
    for n in range(NI):
        if n + 2 < NI:
            s_mm(n + 2)
        exp_pv(n)
        bi, kt = iters[n]
        if kt == NKT - 1:
            finish(bi)
            if bi + 2 < len(blocks):
                load_q(bi + 2)


def phase_D1(C, es):
    nc, S, D = C.nc, C.S, C.D
    SEQ = C.SEQ
    sb = lambda n, s, d=F32: _alloc(C, es, "sb", n, s, d)
    ps = lambda n, s, d=F32: _alloc(C, es, "ps", n, s, d)
    Wdn = sb("D_Wdn", [128, 4, D_MODEL], BF16)
    Wat = sb("D_Wat", [64, 8, D_MODEL], BF16)
    wst = sb("D_wst", [128, 4, D_MODEL], F32)
    dnw = sb("D_dnw", [128, 512], F32)
    npost = sb("D_npost", [128, D_MODEL], F32)
    of_t = [sb("D_of%d" % i, [128, 512], F32) for i in range(2)]
    ob_t = [sb("D_ob%d" % i, [128, 512], F32) for i in range(2)]
    zs_t = [sb("D_zs%d" % i, [128, 512], F32) for i in range(2)]
    xt = [sb("D_xt%d" % i, [128, D_MODEL], F32) for i in range(2)]
    oat = [sb("D_oat%d" % i, [64, 8, 128], BF16) for i in range(2)]
    osum = sb("D_osum", [128, 512], F32)
    sq = sb("D_sq", [128, 512], F32)
    ss4 = sb("D_ss4", [128, 4], F32)
    on = sb("D_on", [128, 512], F32)
    onb = sb("D_onb", [128, 512], BF16)
    odT = sb("D_odT", [128, 4, 128], BF16)
    junk = sb("D_junk", [128, 512], F32)
    ssm = sb("D_ssm", [128, 4], F32)
    tmix = sb("D_tmix", [128, D_MODEL], F32)
    h1 = [sb("D_h1%d" % i, [128, D_MODEL], F32) for i in range(2)]
    psT = ps("D_psT", [128, 4, 128], BF16)
    psM = [ps("D_psM%d" % i, [128, 512], F32) for i in range(4)]
    S.dma("sp", dnw[:], D.dn_norm[:, :], [], ["dnw"], "dnw")
    S.dma("sp", npost[:], D.nmix_post[:, :], [], ["npost"], "npost")
    S.dma("sp", wst[:], D.w_out_dn[:, :, :], [], ["wst"], "wst")
    S.add("dve", lambda e: e.tensor_copy(Wdn[:], wst[:]), ["wst"], ["Wdn"])
    for hh in range(2):
        S.dma("sp", wst[0:64, :, :], D.w_out_at[:, hh * 4:(hh + 1) * 4, :], [], ["wst"], "wst")
        S.add("dve", lambda e, hh=hh: e.tensor_copy(Wat[:, hh * 4:(hh + 1) * 4, :], wst[0:64, :, :]),
              ["wst"], ["Wat"])

    def tile(ti):
        r0 = ti * 128
        e0 = 64 + r0
        p = ti % 2
        k = lambda x: "%s%d" % (x, p)
        S.dma("sp", of_t[p][:], D.OF[0, e0:e0 + 128, :], ["OF"], [k("of")], k("of"))
        S.dma("sp", ob_t[p][:], D.OF[1, e0:e0 + 128, :], ["OF"], [k("ob")], k("ob"))
        S.dma("sp", zs_t[p][:], D.ZS[r0:r0 + 128, :], ["ZS"], [k("zs")], k("zs"))
        S.dma("sp", xt[p][:], D.x[r0:r0 + 128, :], [], [k("xt")], k("xt"))
        S.dma("sp", oat[p][:].rearrange("d (k g) q -> d k g q", k=2),
              D.OAT[:, :, ti, :, :].rearrange("k d g q -> d k g q"), ["OAT"], [k("oat")], k("oat"))
        S.add("pool", lambda e: e.tensor_tensor(out=osum[:], in0=of_t[p][:], in1=ob_t[p][:], op=ALU.add),
              [k("of"), k("ob")], ["osum"])
        S.add("act", lambda e: e.activation(out=sq[:], in_=osum[:], func=AF.Square), ["osum"], ["sq"])
        S.add("dve", lambda e: e.tensor_reduce(out=ss4[:], in_=sq[:].rearrange("p (h d) -> p h d", d=128),
                                               op=ALU.add, axis=AX.X), ["sq"], ["ss4"])
        S.add("act", lambda e: e.activation(out=ss4[:], in_=ss4[:], func=AF.Sqrt, scale=1.0 / 128,
                                            bias=C.eps_t[:, :]), ["ss4"], ["ss4"])
        S.add("dve", lambda e: e.reciprocal(out=ss4[:], in_=ss4[:]), ["ss4"], ["ss4"])
        S.add("dve", lambda e: e.tensor_tensor(
            out=on[:].rearrange("p (h d) -> p h d", d=128), in0=osum[:].rearrange("p (h d) -> p h d", d=128),
            in1=ss4[:].unsqueeze(2).broadcast_to([128, 4, 128]), op=ALU.mult), ["osum", "ss4"], ["on"])
        S.add("pool", lambda e: e.tensor_tensor(out=on[:], in0=on[:], in1=dnw[:], op=ALU.mult),
              ["on", "dnw"], ["on"])
        S.add("dve", lambda e: e.tensor_tensor(out=onb[:], in0=on[:], in1=zs_t[p][:], op=ALU.mult),
              ["on", k("zs")], ["onb"])
        for c in range(4):
            S.add("pe", lambda e, c=c: e.transpose(psT[:, c, :], onb[:, c * 128:(c + 1) * 128], C.ident_b[:, :]),
                  ["onb", "ident_b"], ["psT"])
        S.add("act", lambda e: e.activation(out=odT[:], in_=psT[:], func=AF.Copy), ["psT"], ["odT"])
        for half in range(2):
            pm = psM[p * 2 + half]
            pmk = "psM%d" % (p * 2 + half)
            cs = slice(half * 512, (half + 1) * 512)
            for c in range(4):
                S.add("pe", lambda e, c=c, pm=pm, cs=cs: e.matmul(pm[:, :], odT[:, c, :], Wdn[:, c, cs],
                                                                  start=(c == 0), stop=False),
                      ["odT", "Wdn"], [pmk])
            for h in range(8):
                S.add("pe", lambda e, h=h, pm=pm, cs=cs: e.matmul(pm[:, :], oat[p][:, h, :], Wat[:, h, cs],
                                                                  start=False, stop=(h == 7)),
                      [k("oat"), "Wat"], [pmk])
            S.add("act", lambda e, pm=pm, half=half: e.activation(out=junk[:], in_=pm[:, :], func=AF.Square,
                                                                 accum_out=ssm[:, half:half + 1]),
                  [pmk], ["junk", "ssm%d" % half])
        S.add("dve", lambda e: e.tensor_tensor(out=ssm[:, 2:3], in0=ssm[:, 0:1], in1=ssm[:, 1:2], op=ALU.add),
              ["ssm0", "ssm1"], ["ssm2"])
        S.add("act", lambda e: e.activation(out=ssm[:, 2:3], in_=ssm[:, 2:3], func=AF.Sqrt, scale=1.0 / D_MODEL,
                                            bias=C.eps_t[:, :]), ["ssm2"], ["ssm2"])
        S.add("dve", lambda e: e.reciprocal(out=ssm[:, 3:4], in_=ssm[:, 2:3]), ["ssm2"], ["ssm3"])
        for half in range(2):
            pm = psM[p * 2 + half]
            pmk = "psM%d" % (p * 2 + half)
            cs = slice(half * 512, (half + 1) * 512)
            S.add("dve", lambda e, pm=pm, cs=cs: e.scalar_tensor_tensor(
                out=tmix[:, cs], in0=pm[:, :], scalar=ssm[:, 3:4], in1=npost[:, cs], op0=ALU.mult, op1=ALU.mult),
                [pmk, "ssm3", "npost"], ["tmix"])
        S.add("pool", lambda e: e.tensor_tensor(out=h1[p][:], in0=tmix[:], in1=xt[p][:], op=ALU.add),
              ["tmix", k("xt")], [k("h1")])
        S.dma("pool", D.H1[r0:r0 + 128, :], h1[p][:], [k("h1")], ["H1"], k("h1"))

    for ti in range(SEQ // 128):
        tile(ti)


def phase_D2(C, es):
    nc, S, D = C.nc, C.S, C.D
    SEQ = C.SEQ
    sb = lambda n, s, d=F32: _alloc(C, es, "sb", n, s, d)
    ps = lambda n, s, d=F32: _alloc(C, es, "ps", n, s, d)
    Wup = sb("E_Wup", [128, 8, D_FF], BF16)
    Wdn = sb("E_Wdn", [128, 32, D_MODEL], BF16)
    npre = sb("E_npre", [128, 8], F32)
    npost = sb("E_npost", [128, D_MODEL], F32)
    with ExitStack() as ses:
        wst = _alloc(C, ses, "sb", "E_wst", [128, 4096], F32)
        S.dma("sp", npre[:], D.nmlp_pre[:, :], [], ["npre"], "npre")
        S.dma("sp", npost[:], D.nmlp_post[:, :], [], ["npost"], "npost")
        for kc in range(8):
            S.dma("sp", wst[:], D.w_up[:, kc, :], [], ["wst"], "wst")
            S.add("dve" if kc % 2 == 0 else "pool", lambda e, kc=kc: e.tensor_scalar(
                out=Wup[:, kc, :], in0=wst[:], scalar1=npre[:, kc:kc + 1], scalar2=None, op0=ALU.mult),
                ["wst", "npre"], ["Wup"])
        for c4 in range(8):
            S.dma("sp", wst[:].rearrange("p (c n) -> p c n", c=4), D.w_down[:, c4 * 4:(c4 + 1) * 4, :], [], ["wst"],
                  "wst")
            S.add("act", lambda e, c4=c4: e.activation(
                out=Wdn[:, c4 * 4:(c4 + 1) * 4, :].rearrange("p c n -> p (c n)"), in_=wst[:], func=AF.Copy),
                ["wst"], ["Wdn"])
        S.phase_end()
    h1 = [sb("E_h1%d" % i, [128, D_MODEL], F32) for i in range(4)]
    hb = sb("E_hb", [128, D_MODEL], BF16)
    hT = sb("E_hT", [128, 8, 512], BF16)
    hid = sb("E_hid", [128, 32, 512], BF16)
    rl = [sb("E_rl%d" % i, [128, 512], F32) for i in range(2)]
    junk = sb("E_junk", [128, D_MODEL], BF16)
    st = sb("E_st", [128, 4, 8], F32)
    tf = sb("E_tf", [128, D_MODEL], F32)
    ot = [sb("E_ot0", [128, D_MODEL], F32)] * 2
    pT = ps("E_pT", [128, 8, 128], BF16)
    psU = [ps("E_psU%d" % i, [128, 512], F32) for i in range(2)]
    psD = [ps("E_psD%d" % i, [128, 512], F32) for i in range(4)]

    def pre_tile(st_i, i):
        r0 = st_i * 512 + i * 128
        hk = "h1_%d" % i
        S.dma("sp", h1[i][:], D.H1[r0:r0 + 128, :], ["H1"], [hk], hk)
        S.add("act", lambda e: e.activation(out=junk[:], in_=h1[i][:], func=AF.Square,
                                            accum_out=st[:, i, 0:1]), [hk], ["junk", "st%d" % i])
        S.add("dve", lambda e: e.tensor_scalar(out=st[:, i, 1:2], in0=st[:, i, 0:1], scalar1=1.0 / D_MODEL,
                                               scalar2=EPS, op0=ALU.mult, op1=ALU.add), ["st%d" % i], ["st%d" % i])
        S.add("dve", lambda e: e.reciprocal(out=st[:, i, 1:2], in_=st[:, i, 1:2]), ["st%d" % i], ["st%d" % i])
        S.add("dve", lambda e: e.tensor_tensor(out=st[:, i, 2:3], in0=st[:, i, 1:2], in1=st[:, i, 1:2],
                                               op=ALU.mult), ["st%d" % i], ["st%d" % i])
        S.add("pool", lambda e: e.tensor_copy(hb[:], h1[i][:]), [hk], ["hb"])
        for kc in range(8):
            S.add("pe", lambda e, kc=kc: e.transpose(pT[:, kc, :], hb[:, kc * 128:(kc + 1) * 128], C.ident_b[:, :]),
                  ["hb", "ident_b"], ["pT"])
        S.add("dve", lambda e: e.tensor_copy(hT[:, :, i * 128:(i + 1) * 128], pT[:]), ["pT"], ["hT"])

    def up(fc):
        pu, puk = psU[fc % 2], "psU%d" % (fc % 2)
        r_, rk = rl[fc % 2], "rl%d" % (fc % 2)
        for kc in range(8):
            S.add("pe", lambda e, kc=kc: e.matmul(pu[:, :], Wup[:, kc, fc * 128:(fc + 1) * 128], hT[:, kc, :],
                                                  start=(kc == 0), stop=(kc == 7)), ["Wup", "hT"], [puk])
        S.add("act", lambda e: e.activation(out=r_[:], in_=pu[:, :], func=AF.Relu), [puk], [rk])
        S.add("pool" if fc % 2 == 0 else "dve",
              lambda e: e.tensor_tensor(out=hid[:, fc, :], in0=r_[:], in1=r_[:], op=ALU.mult), [rk], ["hid"])

    def down_tile(st_i, i):
        r0 = st_i * 512 + i * 128
        hk = "h1_%d" % i
        sk = "st%d" % i
        for half in range(2):
            pd, pdk = psD[(i % 2) * 2 + half], "psD%d" % ((i % 2) * 2 + half)
            cs = slice(half * 512, (half + 1) * 512)
            for fc in range(32):
                S.add("pe", lambda e, fc=fc, pd=pd, cs=cs: e.matmul(
                    pd[:, :], hid[:, fc, i * 128:(i + 1) * 128], Wdn[:, fc, cs], start=(fc == 0), stop=(fc == 31)),
                    ["hid", "Wdn"], [pdk])
            S.add("act", lambda e, pd=pd, half=half: e.activation(out=junk[:, 0:512], in_=pd[:, :], func=AF.Square,
                                                                 accum_out=st[:, i, 3 + half:4 + half]),
                  [pdk], ["junk", sk])
        S.add("dve", lambda e: e.tensor_tensor(out=st[:, i, 5:6], in0=st[:, i, 3:4], in1=st[:, i, 4:5], op=ALU.add),
              [sk], [sk])
        S.add("dve", lambda e: e.tensor_tensor(out=st[:, i, 5:6], in0=st[:, i, 5:6], in1=st[:, i, 2:3], op=ALU.mult),
              [sk], [sk])
        S.add("act", lambda e: e.activation(out=st[:, i, 5:6], in_=st[:, i, 5:6], func=AF.Sqrt, scale=1.0 / D_MODEL,
                                            bias=C.eps_t[:, :]), [sk], [sk])
        S.add("dve", lambda e: e.reciprocal(out=st[:, i, 6:7], in_=st[:, i, 5:6]), [sk], [sk])
        S.add("dve", lambda e: e.tensor_tensor(out=st[:, i, 7:8], in0=st[:, i, 6:7], in1=st[:, i, 1:2], op=ALU.mult),
              [sk], [sk])
        o_ = ot[0]
        okk = "ot0"
        for half in range(2):
            pd, pdk = psD[(i % 2) * 2 + half], "psD%d" % ((i % 2) * 2 + half)
            cs = slice(half * 512, (half + 1) * 512)
            S.add("dve", lambda e, pd=pd, cs=cs: e.scalar_tensor_tensor(
                out=tf[:, cs], in0=pd[:, :], scalar=st[:, i, 7:8], in1=npost[:, cs], op0=ALU.mult, op1=ALU.mult),
                [pdk, sk, "npost"], ["tf"])
        S.add("pool", lambda e: e.tensor_tensor(out=o_[:], in0=tf[:], in1=h1[i][:], op=ALU.add), ["tf", hk], [okk])
        S.dma("pool", D.out[r0:r0 + 128, :], o_[:], [okk], ["out"], okk)

    for st_i in range(SEQ // 512):
        for i in range(4):
            pre_tile(st_i, i)
        for fc in range(32):
            up(fc)
        for i in range(4):
            down_tile(st_i, i)


def _consts(SEQ):
    r = (np.arange(SEQ) // 64).astype(np.float64)
    c = (np.arange(SEQ) % 64).astype(np.float64)
    F = 16
    freqs = (10000.0 ** (-np.arange(F, dtype=np.float32) / F)).astype(np.float32)
    ang = np.concatenate([r[:, None].astype(np.float32) * freqs, c[:, None].astype(np.float32) * freqs],
                         axis=-1).astype(np.float32)
    cos = np.cos(ang.astype(np.float64)).astype(np.float32)
    sin = np.sin(ang.astype(np.float64)).astype(np.float32)
    cos8 = np.ascontiguousarray(np.tile(cos, (1, 8)))
    sin8 = np.ascontiguousarray(np.tile(sin, (1, 8)))
    i = np.arange(64)
    BIG = 30000.0
    mm_, xx_ = i[:, None], i[None, :]
    m = np.zeros((64, 12, 64), np.float32)
    m[:, 0, :] = (mm_ <= xx_)
    m[:, 1, :] = (mm_ > xx_)
    m[:, 2, :] = (mm_ >= xx_)
    m[:, 3, :] = (mm_ < xx_)
    m[:, 4, :] = -BIG * (mm_ < xx_)
    m[:, 5, :] = -BIG * (mm_ > xx_)
    m[:, 6, :] = -BIG * (mm_ <= xx_)
    m[:, 7, :] = -BIG * (mm_ >= xx_)
    for h in range(4):
        m[:, 8 + h, :] = np.eye(64)
    return {"ident": np.eye(128, dtype=np.float32), "rope_cos": cos8, "rope_sin": sin8, "masks": m}


def prep_shared(inp, SEQ):
    f = lambda a: np.ascontiguousarray(a, dtype=np.float32)
    bc = lambda v, n=128: f(np.broadcast_to(np.asarray(v).reshape(1, -1), (n, np.asarray(v).size)))
    d = {}
    d["meta"] = f(inp["meta_tokens"])
    d["w_in"] = f(inp["w_in"][0].reshape(8, 128, IN_COLS).transpose(1, 0, 2))
    d["nmix_pre"] = f(inp["norm_mix_pre"][0].reshape(8, 128).T)
    d["conv_w"] = f(inp["conv_w"][0].reshape(5, 12, 128).transpose(2, 1, 0))
    d["a_log"] = bc(inp["a_log"][0].reshape(-1))
    d["dt_bias"] = bc(inp["dt_bias"][0].reshape(-1))
    d["dn_norm"] = bc(np.tile(inp["dn_out_norm"][0], 4))
    d["q_norm"] = bc(np.tile(inp["q_norm"][0], 8))
    d["k_norm"] = bc(np.tile(inp["k_norm"][0], 2))
    wo = inp["w_out"][0]
    d["w_out_dn"] = f(wo[0:512].reshape(4, 128, D_MODEL).transpose(1, 0, 2))
    d["w_out_at"] = f(wo[512:1024].reshape(8, 64, D_MODEL).transpose(1, 0, 2))
    d["nmix_post"] = bc(inp["norm_mix_post"][0])
    d["nmlp_pre"] = f(inp["norm_mlp_pre"][0].reshape(8, 128).T)
    d["nmlp_post"] = bc(inp["norm_mlp_post"][0])
    d["w_up"] = f(inp["w_up"][0].reshape(8, 128, D_FF).transpose(1, 0, 2))
    d["w_down"] = f(inp["w_down"][0].reshape(32, 128, D_MODEL).transpose(1, 0, 2))
    d.update(_consts(SEQ))
    return d


_NC_CACHE = {}


def kernel(**inputs):
    x = np.asarray(inputs["x"])
    B, SEQ, _ = x.shape
    if SEQ not in _NC_CACHE:
        _NC_CACHE[SEQ] = build(SEQ)
    nc = _NC_CACHE[SEQ]
    shared = prep_shared(inputs, SEQ)
    in_maps = []
    for b in range(B):
        m = dict(shared)
        m["x"] = np.ascontiguousarray(x[b], dtype=np.float32)
        in_maps.append(m)
    res = run_bass_kernel_spmd(nc, in_maps, core_ids=list(range(B)))
    out = np.stack([np.asarray(r["out"]) for r in res.results], axis=0)
    return out.astype(np.float32)
```
